# Optimizing a Trainium2 kernel written in Bass

```python
import math
import jax, jax.numpy as jnp
from jax import lax
import numpy as np

D_MODEL = 2048
BATCH = 4
SEQ = 4096
DEPTH = 2

N_MIXERS = 2
D_FF = 4 * D_MODEL
CONV_WIDTH = 3
N_HEADS = 16
HEAD_DIM_V = D_MODEL // N_HEADS
HEAD_DIM_QK = HEAD_DIM_V // 2
QBLK = 128
ATTN_SCALE = HEAD_DIM_QK ** -0.5
NORM_EPS = 1e-6
SUBLN_EPS = 1e-5
N_CONV_LAYERS = (DEPTH + 1) // 2
N_ATTN_LAYERS = DEPTH // 2

kernel_name = "hybrid_shortconv_diffattn_encoder"


def _rmsnorm(x, g, eps=NORM_EPS):
    xf = x.astype(jnp.float32)
    y = xf * lax.rsqrt(jnp.mean(xf * xf, axis=-1, keepdims=True) + eps)
    return (y * g.astype(jnp.float32)).astype(x.dtype)


def _alibi_slopes():
    h = jnp.arange(1, N_HEADS + 1, dtype=jnp.float32)
    return jnp.exp2(-8.0 * h / N_HEADS)


def _lambda_init(layer_idx):
    return 0.8 - 0.6 * math.exp(-0.3 * layer_idx)


def _short_conv_mixer(h, w_in, w_conv, w_out):
    bcu = h @ w_in
    gate_b, gate_c, u = jnp.split(bcu, 3, axis=-1)
    u = gate_c * u
    up = jnp.pad(u, ((0, 0), (1, 1), (0, 0)))
    conv = w_conv[0] * up[:, :-2] + w_conv[1] * up[:, 1:-1] + w_conv[2] * up[:, 2:]
    return (gate_b * conv) @ w_out


def _diff_attention(h, w_qkv, lq1, lk1, lq2, lk2, subln, w_o, lambda_init):
    bsz, seq, _ = h.shape
    qkv = h @ w_qkv
    q, k, v = jnp.split(qkv, 3, axis=-1)
    q = q.reshape(bsz, seq, N_HEADS, 2, HEAD_DIM_QK)
    k = k.reshape(bsz, seq, N_HEADS, 2, HEAD_DIM_QK)
    v = v.reshape(bsz, seq, N_HEADS, HEAD_DIM_V)
    f32 = jnp.float32
    lam = (jnp.exp(jnp.sum(lq1.astype(f32) * lk1.astype(f32)))
           - jnp.exp(jnp.sum(lq2.astype(f32) * lk2.astype(f32))) + lambda_init)
    slopes = _alibi_slopes()
    kpos = jnp.arange(seq, dtype=jnp.int32)
    n_blk = seq // QBLK
    q_blocks = q.reshape(bsz, n_blk, QBLK, N_HEADS, 2, HEAD_DIM_QK).transpose(1, 0, 2, 3, 4, 5)
    q_pos = kpos.reshape(n_blk, QBLK)

    def block(args):
        qb, pos = args
        s = jnp.einsum('bqhcd,bkhcd->bchqk', qb, k).astype(f32) * ATTN_SCALE
        dist = jnp.abs(pos[:, None] - kpos[None, :]).astype(f32)
        bias = -slopes[:, None, None] * dist
        p = jax.nn.softmax(s + bias, axis=-1)
        a = p[:, 0] - lam * p[:, 1]
        return jnp.einsum('bhqk,bkhd->bqhd', a.astype(v.dtype), v)

    o = lax.map(block, (q_blocks, q_pos))
    o = o.transpose(1, 0, 2, 3, 4).reshape(bsz, seq, N_HEADS, HEAD_DIM_V)
    o = _rmsnorm(o, subln, SUBLN_EPS) * (1.0 - lambda_init)
    return o.reshape(bsz, seq, D_MODEL) @ w_o


def _sqrelu_mlp(h, w1, w2):
    a = jax.nn.relu(h @ w1)
    return (a * a) @ w2


def setup_inputs(seed: int = 0) -> dict:
    key = jax.random.key(seed)
    ks = jax.random.split(key, 20)
    f32 = jnp.float32
    D, F = D_MODEL, D_FF
    nrm = lambda k, shape, scale: jax.random.normal(k, shape, f32) * scale
    return {
        "x": nrm(ks[0], (BATCH, SEQ, D), 1.0),
        "ln_mix": 1.0 + nrm(ks[1], (DEPTH, D), 0.02),
        "ln_mlp": 1.0 + nrm(ks[2], (DEPTH, D), 0.02),
        "conv_w_in": nrm(ks[3], (N_CONV_LAYERS, D, 3 * D), D ** -0.5),
        "conv_w": nrm(ks[4], (N_CONV_LAYERS, CONV_WIDTH, D), CONV_WIDTH ** -0.5),
        "conv_w_out": nrm(ks[5], (N_CONV_LAYERS, D, D), D ** -0.5),
        "attn_w_qkv": nrm(ks[6], (N_ATTN_LAYERS, D, 3 * D), D ** -0.5),
        "attn_lambda_q1": nrm(ks[7], (N_ATTN_LAYERS, HEAD_DIM_QK), 0.1),
        "attn_lambda_k1": nrm(ks[8], (N_ATTN_LAYERS, HEAD_DIM_QK), 0.1),
        "attn_lambda_q2": nrm(ks[9], (N_ATTN_LAYERS, HEAD_DIM_QK), 0.1),
        "attn_lambda_k2": nrm(ks[10], (N_ATTN_LAYERS, HEAD_DIM_QK), 0.1),
        "attn_subln": 1.0 + nrm(ks[11], (N_ATTN_LAYERS, HEAD_DIM_V), 0.02),
        "attn_w_o": nrm(ks[12], (N_ATTN_LAYERS, D, D), D ** -0.5),
        "mlp_w1": nrm(ks[13], (DEPTH, D, F), D ** -0.5),
        "mlp_w2": nrm(ks[14], (DEPTH, F, D), F ** -0.5),
        "ln_f": 1.0 + nrm(ks[15], (D,), 0.02),
    }


def reference(x, ln_mix, ln_mlp, conv_w_in, conv_w, conv_w_out, attn_w_qkv,
              attn_lambda_q1, attn_lambda_k1, attn_lambda_q2, attn_lambda_k2,
              attn_subln, attn_w_o, mlp_w1, mlp_w2, ln_f):
    h = x
    for i in range(DEPTH):
        j = i // N_MIXERS
        hn = _rmsnorm(h, ln_mix[i])
        if i % N_MIXERS == 0:
            mix = _short_conv_mixer(hn, conv_w_in[j], conv_w[j], conv_w_out[j])
        else:
            mix = _diff_attention(hn, attn_w_qkv[j], attn_lambda_q1[j], attn_lambda_k1[j],
                                  attn_lambda_q2[j], attn_lambda_k2[j], attn_subln[j],
                                  attn_w_o[j], _lambda_init(i))
        h = h + mix
        h = h + _sqrelu_mlp(_rmsnorm(h, ln_mlp[i]), mlp_w1[i], mlp_w2[i])
    return _rmsnorm(h, ln_f)
```

```python
import numpy as np
from contextlib import ExitStack
import concourse.bass as bass
import concourse.mybir as mybir
from concourse.bass_utils import run_bass_kernel_spmd

F32 = mybir.dt.float32
BF16 = mybir.dt.bfloat16
I32 = mybir.dt.int32
ALU = mybir.AluOpType
AF = mybir.ActivationFunctionType

D = 2048
KC = 16
NT = 4
TW = 512
TWH = 514
NCORE = 8
SEQ = 4096
HALF = 2048
NH = 16
EPS = 1e-6
SUBEPS = 1e-5
LAMBDA_INIT1 = 0.8 - 0.6 * float(np.exp(-0.3 * 1))
TOFF = 3968
TWID = 6016
NDMASEM = 8

P_WIN, P_WOUT, P_MLP0, P_QKV, P_WO, P_MLP1 = 0, 48, 64, 192, 240, 256
NPAN = 384


class Item:
    __slots__ = ("eng", "fn", "deps", "needed", "val", "sem", "is_dma", "inc")

    def __init__(self, eng, fn, deps, is_dma):
        self.eng, self.fn, self.deps, self.is_dma = eng, fn, deps, is_dma
        self.needed = False
        self.val = 0
        self.sem = None
        self.inc = 16


class Rec:
    ENGS = ("sp", "act", "pe", "dve", "pool")

    def __init__(self):
        self.q = {e: [] for e in self.ENGS}
        self.lastw = {}
        self.reads = {}
        self.dma_rot = {e: 0 for e in self.ENGS}
        self.dma_last = {}
        self.dma_cnt = {}

    def op(self, eng, fn, r=(), w=(), dma=False, cc=False, after=()):
        deps = set()
        for k in after:
            t = self.lastw.get(k)
            if t is not None:
                deps.add(t)
        for k in r:
            t = self.lastw.get(k)
            if t is not None:
                deps.add(t)
        for k in w:
            t = self.lastw.get(k)
            if t is not None:
                deps.add(t)
            for t in self.reads.get(k, ()):
                deps.add(t)
        if eng == "pe" and not dma:
            deps = {d for d in deps if d.is_dma or d.eng != "pe"}
        it = Item(eng, fn, deps, dma)
        if cc:
            it.sem = ("cc", 0)
            self.dma_cnt[it.sem] = self.dma_cnt.get(it.sem, 0) + 1
            it.val = self.dma_cnt[it.sem]
            it.inc = 1
            self.dma_last[it.sem] = it
        elif dma:
            j = self.dma_rot[eng] % NDMASEM
            self.dma_rot[eng] += 1
            sem = (eng, j)
            prev = self.dma_last.get(sem)
            if prev is not None:
                deps.add(prev)
            self.dma_last[sem] = it
            self.dma_cnt[sem] = self.dma_cnt.get(sem, 0) + 16
            it.sem = sem
            it.val = self.dma_cnt[sem]
        deps.discard(it)
        for d in deps:
            d.needed = True
        self.q[eng].append(it)
        for k in r:
            lst = self.reads.setdefault(k, [])
            if not dma:
                for i_, o in enumerate(lst):
                    if o.eng == eng and not o.is_dma:
                        lst[i_] = it
                        break
                else:
                    lst.append(it)
            else:
                lst.append(it)
        for k in w:
            self.lastw[k] = it
            self.reads[k] = []
        return it

    def emit(self, nc, block, semh):
        for eng in self.ENGS:
            cnt = 0
            for it in self.q[eng]:
                if not it.is_dma and it.needed:
                    cnt += 1
                    it.sem = eng
                    it.val = cnt
        bname = {"sp": "sync", "act": "scalar", "pe": "tensor", "dve": "vector", "pool": "gpsimd"}
        for eng in self.ENGS:
            items = self.q[eng]

            def body(e, items=items, eng=eng):
                known = {}
                for it in items:
                    waits = {}
                    for d in it.deps:
                        if eng == "pe" and d.eng == "pe" and not d.is_dma:
                            continue
                        if waits.get(d.sem, 0) < d.val:
                            waits[d.sem] = d.val
                    for s, v in waits.items():
                        if known.get(s, 0) >= v:
                            continue
                        known[s] = v
                        e.wait_ge(semh[s], v)
                    if it.fn is None:
                        continue
                    ins = it.fn(e)
                    if it.is_dma:
                        ins.then_inc(semh[it.sem], it.inc)
                    elif it.needed:
                        ins.then_inc(semh[eng], 1)

            getattr(block, bname[eng])(body)


def build(mode):
    doA = mode in ("A", "ALL")
    doB = mode in ("B", "ALL")
    NTL = DEBUG["NT"]
    STOP = DEBUG["stop"]
    nc = bass.Bass("TRN2", target_bir_lowering=False)
    R = Rec()

    def dt_in(name, shape, dt=F32):
        return nc.dram_tensor(name, list(shape), dt, kind="ExternalInput").ap()

    def dt_out(name, shape, dt=F32):
        return nc.dram_tensor(name, list(shape), dt, kind="ExternalOutput").ap()

    def dt_int(name, shape, dt=F32):
        return nc.dram_tensor(name, list(shape), dt).ap()

    p_lo = 0 if doA else P_WO
    p_hi = NPAN if doB else P_WO
    npan = p_hi - p_lo
    wpan = dt_in("wpan", [npan, 128, 2048])
    wbf = dt_int("wbf", [npan, 128, 2048], BF16)
    gains = dt_in("gains", [128, 5 * KC])
    smalls = dt_in("smalls", [128, 3 * KC + 1 + 1])
    if doB:
        lamp = dt_in("lamp", [4, 64])
    if doA:
        xT = dt_in("xT", [NT, 128, KC * TWH])
    if mode == "A":
        h1s = dt_out("h1s", [NT, 128, KC * TW])
        Qs = dt_out("Qs", [NH, 128, HALF], BF16)
        Ks = dt_out("Ks", [NH, 128, HALF], BF16)
        Vs = dt_out("Vs", [NH, 128, KC * 128], BF16)
    elif mode == "B":
        h1s = dt_in("h1s", [NT, 128, KC * TW])
        Qs = dt_in("Qs", [NH, 128, HALF], BF16)
        Kf = dt_in("Kf", [2, NH, 128, HALF], BF16)
        Vf = dt_in("Vf", [2, NH, 128, KC * 128], BF16)
    else:
        h1s = dt_int("h1s", [NT, 128, KC * TW])
        Qs = dt_int("Qs", [NH, 128, HALF], BF16)
        KVs = dt_int("KVs", [NH, 2, 128, HALF], BF16)
        KVf = dt_int("KVf", [NH, 2, 2 * 128, HALF], BF16)
        Ks = [KVs[h, 0] for h in range(NH)]
        Vs = [KVs[h, 1] for h in range(NH)]
    if doB:
        outT = dt_out("outT", [NT, 128, KC * TW])

    es = ExitStack()
    with es:
        ARENA = 212480 // 2
        arena = es.enter_context(nc.sbuf_tensor("arena", [128, ARENA], BF16))
        off = [0]

        def carve(nelem, dt):
            nb = nelem * (4 if dt in (F32, I32) else 2)
            nb = (nb + 63) // 64 * 64
            o = off[0]
            off[0] += nb // 2
            assert off[0] <= ARENA, ("sbuf overflow", off[0] * 2)
            v = arena[:, o:o + nb // 2]
            if dt != BF16:
                v = v.bitcast(dt)
            return v[:, 0:nelem]

        xh = carve(KC * TWH, F32).rearrange("p (k n) -> p k n", k=KC)
        hn = carve(KC * TWH, BF16).rearrange("p (k n) -> p k n", k=KC)
        abuf = carve(32 * TWH, BF16).rearrange("p (k n) -> p k n", k=32)
        wring = [carve(2048, BF16).rearrange("p (k n) -> p k n", k=KC) for _ in range(4)]
        st_o = off[0]
        Tt = carve(TWID, F32)
        Ti = arena[:, st_o:st_o + 2 * TWID].bitcast(I32)
        st32 = [arena[:, st_o + i * 2048:st_o + (i + 1) * 2048].bitcast(F32) for i in range(2)]
        st16 = [arena[:, st_o + 4096 + i * 1024:st_o + 4096 + (i + 1) * 1024] for i in range(2)]
        rstd = carve(TWH, F32)
        csb = [carve(TWH, F32) for _ in range(2)]
        upb = [carve(TWH, F32) for _ in range(2)]
        accb = [carve(TW, F32) for _ in range(2)]
        rbuf = [carve(TW, F32) for _ in range(2)]
        qkvb = [carve(TW, BF16) for _ in range(3)]
        vtok = [carve(TW, BF16) for _ in range(2)]
        kTb = [carve(SEQ, BF16) for _ in range(2)]
        vhb = [carve(SEQ, BF16).rearrange("p (k n) -> p k n", k=32) for _ in range(2)]
        qTo = [off[0], off[0] + 2 * TW]
        qTb = [[carve(TW, BF16) for _ in range(2)] for _ in range(2)]
        qT2 = [arena[:, qTo[p_]:qTo[p_] + 2 * TW].rearrange("p (c n) -> p c n", c=2) for p_ in range(2)]
        scb = [carve(TW, F32) for _ in range(3)]
        Eb = [carve(TW, BF16) for _ in range(5)]
        rzb = carve(TW, F32)
        tb = carve(TW, F32)
        ohb = [carve(TW, F32) for _ in range(2)]
        osq = carve(TW, BF16)
        ones = carve(128, BF16)
        ident = carve(128, BF16)
        gsb = carve(5 * KC, F32)
        smb = carve(3 * KC + 2, F32)
        lamb = carve(4 * 64, F32).rearrange("p (a n) -> p a n", a=4)
        lamt = carve(2 * 64, F32).rearrange("p (a n) -> p a n", a=2)
        lams = carve(4, F32)
        cols = carve(8, F32)
        iot = tb.bitcast(I32)[:, 0:128]
        iotf = rzb[:, 0:128]
        ST_KEYS = [("st32", 0), ("st32", 1), ("st16", 0), ("st16", 1)]

        banks = [es.enter_context(nc.psum_tensor(f"bank{i}", [128, TW], F32))[:] for i in range(8)]
        semh = {}
        for e in Rec.ENGS:
            semh[e] = es.enter_context(nc.semaphore(f"s_{e}"))
        for e in ("sp", "pool", "act"):
            for j in range(NDMASEM):
                semh[(e, j)] = es.enter_context(nc.semaphore(f"d_{e}{j}"))

        semh[("cc", 0)] = es.enter_context(nc.semaphore("s_cc"))
        zero_c, eps_c, seps_c, nlam_c, subg_c, shift_c = (cols[:, i:i + 1] for i in range(6))

        R.op("pool", lambda e: e.memset(ones, 1.0), w=[("ones",)])
        R.op("pool", lambda e: e.memset(cols, 0.0), w=[("cols",)])
        R.op("pool", lambda e: e.memset(eps_c, EPS), r=[("cols",)], w=[("cols", 1)])
        R.op("pool", lambda e: e.memset(seps_c, SUBEPS), r=[("cols",)], w=[("cols", 2)])
        R.op("pool", lambda e: e.iota(iot, [[1, 128]], base=0, channel_multiplier=-1), w=[("tb",)])
        R.op("pool", lambda e: e.tensor_copy(out=iotf, in_=iot), r=[("tb",)], w=[("rz",)])
        R.op("pool", lambda e: e.tensor_scalar(out=ident, in0=iotf, scalar1=0.0, scalar2=None, op0=ALU.is_equal),
             r=[("rz",)], w=[("ident",)])
        R.op("sp", lambda e: e.dma_start(out=gsb, in_=gains), w=[("gsb",)], dma=True)
        R.op("sp", lambda e: e.dma_start(out=smb, in_=smalls), w=[("smb",)], dma=True)
        CONST_KEYS = [("cols",), ("cols", 1), ("cols", 2), ("gsb",), ("smb",), ("ones",), ("ident",)]

        NCH = npan

        def pp_advance(upto):
            pass

        n_direct = (P_WO - p_lo) if doA else 0
        bg_state = {"done": False}

        bg_state["next"] = n_direct
        bg_state["armed"] = False
        BG_EVERY = 4

        def bg_one(pace_key=None):
            pid_ = bg_state["next"]
            if pid_ >= npan:
                return
            bg_state["next"] += 1
            R.op("pool", lambda e, pid_=pid_: e.dma_start(out=wbf[pid_], in_=wpan[pid_]),
                 after=([pace_key] if pace_key is not None else []), w=[("wbf", pid_)], dma=True)

        def emit_background():
            bg_state["armed"] = True

        def bg_flush():
            while bg_state["next"] < npan:
                bg_one()

        if n_direct == 0:
            bg_flush()
        direct_done = set()

        seq = []
        if doA:
            for t in range(NTL):
                if STOP >= 2:
                    seq += list(range(P_WIN, P_WIN + 48))
                if STOP >= 3:
                    seq += list(range(P_WOUT, P_WOUT + 16))
                if STOP >= 4:
                    seq += list(range(P_MLP0, P_MLP0 + 128))
                if STOP >= 5:
                    seq += list(range(P_QKV, P_QKV + 48))
        if doB and DEBUG["Bstage"] >= 2:
            for t in range(NTL):
                seq += list(range(P_WO, P_WO + 16)) + list(range(P_MLP1, P_MLP1 + 128))
        seq = [p - p_lo for p in seq]
        wst = {"ld": 0, "use": 0}

        def w_prefetch(upto):
            upto = min(upto, len(seq))
            while wst["ld"] < upto:
                i = wst["ld"]
                pid = seq[i]
                if pid < n_direct and pid not in direct_done:
                    direct_done.add(pid)
                    R.op("pool", lambda e, i=i, pid=pid: e.dma_start(
                        out=wring[i % 4], in_=wpan[pid].rearrange("p (k n) -> p k n", k=KC)),
                        w=[("w", i % 4)], dma=True)
                    R.op("sp", lambda e, i=i, pid=pid: e.dma_start(
                        out=wbf[pid].rearrange("p (k n) -> p k n", k=KC), in_=wring[i % 4]),
                        r=[("w", i % 4)], w=[("wbf", pid)], dma=True)
                    if len(direct_done) == n_direct:
                        emit_background()
                else:
                    R.op("sp", lambda e, i=i, pid=pid: e.dma_start(
                        out=wring[i % 4], in_=wbf[pid].rearrange("p (k n) -> p k n", k=KC)),
                        r=[("wbf", pid)], w=[("w", i % 4)], dma=True)
                wst["ld"] += 1

        def w_next(expect):
            i = wst["use"]
            assert seq[i] == expect - p_lo, (i, seq[i], expect)
            w_prefetch(i + 4)
            wst["use"] += 1
            if bg_state["armed"] and i % BG_EVERY == 0:
                bg_one(("w", i % 4))
            return wring[i % 4], ("w", i % 4)

        psrot = [0]

        def ps_next():
            b = psrot[0] % 5
            psrot[0] += 1
            return banks[b], ("ps", b)

        def mm_group(pan_ids, rhs_fn, rhs_keys, ps, pskey, extra=None):
            n = len(pan_ids) * KC
            idx = 0
            for pi, pid in enumerate(pan_ids):
                wt, wkey = w_next(pid)
                for kc in range(KC):
                    gk = pi * KC + kc
                    R.op("pe", lambda e, wt=wt, kc=kc, gk=gk, idx=idx: e.matmul(
                        ps, lhsT=wt[:, kc, :], rhs=rhs_fn(gk), start=(idx == 0), stop=(idx == n - 1)),
                        r=[wkey, rhs_keys(gk)], w=[pskey])
                    if extra is not None:
                        extra(wt, wkey, kc, gk, idx, n)
                    idx += 1

        def rmsnorm(gi, halo, src_keys=None):
            c0, c1 = (0, TWH) if halo else (1, TW + 1)
            for kc in range(KC):
                R.op("act", lambda e, kc=kc: e.activation(out=abuf[:, 16 + kc, c0:c1], in_=xh[:, kc, c0:c1],
                                                          func=AF.Square, bias=zero_c, scale=1.0),
                     r=[("xh", kc), ("cols",)], w=[("a", 16 + kc)])
            for kc in range(KC):
                R.op("pe", lambda e, kc=kc: e.matmul(banks[6], lhsT=ones, rhs=abuf[:, 16 + kc, 1:TW + 1],
                                                     start=(kc == 0), stop=(kc == KC - 1)),
                     r=[("ones",), ("a", 16 + kc)], w=[("ps", 6)])
                if halo:
                    R.op("pe", lambda e, kc=kc: e.matmul(banks[7][:, 0:2], lhsT=ones, rhs=abuf[:, 16 + kc, 0:TWH:TW + 1],
                                                         start=(kc == 0), stop=(kc == KC - 1)),
                         r=[("ones",), ("a", 16 + kc)], w=[("ps", 7)])
            R.op("act", lambda e: e.activation(out=rstd[:, 1:TW + 1], in_=banks[6], func=AF.Sqrt, bias=eps_c, scale=1.0 / D),
                 r=[("ps", 6), ("cols", 1)], w=[("rstd",)])
            if halo:
                R.op("act", lambda e: e.activation(out=rstd[:, 0:TWH:TW + 1], in_=banks[7][:, 0:2], func=AF.Sqrt,
                                                   bias=eps_c, scale=1.0 / D),
                     r=[("ps", 7), ("cols", 1)], w=[("rstd",)])
            R.op("dve", lambda e: e.reciprocal(out=rstd[:, c0:c1], in_=rstd[:, c0:c1]),
                 r=[("rstd",)], w=[("rstd",)])
            for kc in range(KC):
                R.op("dve", lambda e, kc=kc: e.scalar_tensor_tensor(
                    out=hn[:, kc, c0:c1], in0=xh[:, kc, c0:c1], scalar=gsb[:, gi * KC + kc:gi * KC + kc + 1],
                    in1=rstd[:, c0:c1], op0=ALU.mult, op1=ALU.mult),
                    r=[("xh", kc), ("gsb",), ("rstd",)], w=[("hn", kc)])

        def resid_linear(pbase, npan_per, rhs_fn, rhs_keys):
            for do in range(KC):
                ps, pk = ps_next()
                mm_group([pbase + do * npan_per + s for s in range(npan_per)], rhs_fn, rhs_keys, ps, pk)
                R.op("dve", lambda e, do=do, ps=ps: e.tensor_tensor(out=xh[:, do, 1:TW + 1], in0=ps,
                                                                    in1=xh[:, do, 1:TW + 1], op=ALU.add),
                     r=[pk, ("xh", do)], w=[("xh", do)])

        def mlp(pbase):
            for half in range(2):
                hb = pbase + half * 64
                for fo in range(32):
                    ps, pk = ps_next()
                    mm_group([hb + fo], lambda gk: hn[:, gk, 1:TW + 1], lambda gk: ("hn", gk), ps, pk)
                    rb = rbuf[fo % 2]
                    R.op("act", lambda e, ps=ps, rb=rb: e.activation(out=rb, in_=ps, func=AF.Relu, bias=zero_c, scale=1.0),
                         r=[pk, ("cols",)], w=[("rbuf", fo % 2)])
                    R.op("dve", lambda e, fo=fo, rb=rb: e.tensor_tensor(out=abuf[:, fo, 0:TW], in0=rb, in1=rb, op=ALU.mult),
                         r=[("rbuf", fo % 2)], w=[("a", fo)])
                resid_linear(hb + 32, 2, lambda gk: abuf[:, gk, 0:TW], lambda gk: ("a", gk))

        if doA:
            cw = smb
            if STOP < 99:
                dbg_x = dt_out("dbg_x", [128, KC * TWH])
                dbg_h = dt_out("dbg_h", [128, KC * TWH], BF16)
                dbg_a = dt_out("dbg_a", [128, 32 * TWH], BF16)
            for t in range(NTL):
                R.op("sp", lambda e, t=t: e.dma_start(out=xh, in_=xT[t].rearrange("p (k n) -> p k n", k=KC)),
                     w=[("xh", kc) for kc in range(KC)], dma=True)
                if STOP >= 1:
                    rmsnorm(0, True)
                for j in range(KC if STOP >= 2 else 0):
                    ps_c, pk_c = ps_next()
                    ps_u, pk_u = ps_next()
                    ps_b, pk_b = ps_next()
                    cs, up, acc = csb[j % 2], upb[j % 2], accb[j % 2]

                    def halo_mm(hb):
                        def f(wt, wkey, kc, gk, idx, n):
                            R.op("pe", lambda e: e.matmul(banks[hb][:, 0:2], lhsT=wt[:, kc, :],
                                                          rhs=hn[:, kc, 0:TWH:TW + 1], start=(idx == 0), stop=(idx == n - 1)),
                                 r=[wkey, ("hn", kc)], w=[("ps", hb)])
                        return f
                    rf = lambda gk: hn[:, gk, 1:TW + 1]
                    rk = lambda gk: ("hn", gk)
                    mm_group([P_WIN + 3 * j], rf, rk, ps_c, pk_c, extra=halo_mm(7))
                    R.op("act", lambda e, cs=cs, ps_c=ps_c: e.copy(out=cs[:, 1:TW + 1], in_=ps_c),
                         r=[pk_c, ("cols",)], w=[("cs", j % 2)])
                    R.op("act", lambda e, cs=cs: e.copy(out=cs[:, 0:TWH:TW + 1], in_=banks[7][:, 0:2]),
                         r=[("ps", 7), ("cols",)], w=[("csh", j % 2)])
                    mm_group([P_WIN + 3 * j + 1], rf, rk, ps_u, pk_u, extra=halo_mm(5))
                    R.op("dve", lambda e, cs=cs, up=up, ps_u=ps_u: e.tensor_tensor(
                        out=up[:, 1:TW + 1], in0=ps_u, in1=cs[:, 1:TW + 1], op=ALU.mult),
                        r=[pk_u, ("cs", j % 2)], w=[("up", j % 2)])
                    R.op("dve", lambda e, cs=cs, up=up: e.tensor_tensor(
                        out=up[:, 0:TWH:TW + 1], in0=banks[5][:, 0:2], in1=cs[:, 0:TWH:TW + 1], op=ALU.mult),
                        r=[("ps", 5), ("csh", j % 2)], w=[("uph", j % 2)])
                    R.op("dve", lambda e, up=up, acc=acc, j=j: e.tensor_scalar(
                        out=acc, in0=up[:, 1:TW + 1], scalar1=cw[:, KC + j:KC + j + 1], scalar2=None, op0=ALU.mult),
                        r=[("up", j % 2), ("smb",)], w=[("acc", j % 2)])
                    R.op("dve", lambda e, up=up, acc=acc, j=j: e.scalar_tensor_tensor(
                        out=acc, in0=up[:, 0:TW], scalar=cw[:, j:j + 1], in1=acc, op0=ALU.mult, op1=ALU.add),
                        r=[("up", j % 2), ("uph", j % 2), ("smb",), ("acc", j % 2)], w=[("acc", j % 2)])
                    R.op("dve", lambda e, up=up, acc=acc, j=j: e.scalar_tensor_tensor(
                        out=acc, in0=up[:, 2:TW + 2], scalar=cw[:, 2 * KC + j:2 * KC + j + 1], in1=acc,
                        op0=ALU.mult, op1=ALU.add),
                        r=[("up", j % 2), ("uph", j % 2), ("smb",), ("acc", j % 2)], w=[("acc", j % 2)])
                    mm_group([P_WIN + 3 * j + 2], rf, rk, ps_b, pk_b)
                    R.op("dve", lambda e, acc=acc, ps_b=ps_b, j=j: e.tensor_tensor(
                        out=abuf[:, j, 0:TW], in0=ps_b, in1=acc, op=ALU.mult),
                        r=[pk_b, ("acc", j % 2)], w=[("a", j)])
                if STOP >= 3:
                    resid_linear(P_WOUT, 1, lambda gk: abuf[:, gk, 0:TW], lambda gk: ("a", gk))
                if STOP >= 4:
                    rmsnorm(1, False)
                    mlp(P_MLP0)
                if STOP < 5:
                    allk = [("xh", kc) for kc in range(KC)] + [("hn", kc) for kc in range(KC)] + [("a", kc) for kc in range(32)]
                    R.op("sp", lambda e: e.dma_start(out=dbg_x.rearrange("p (k n) -> p k n", k=KC), in_=xh), r=allk, w=[("dbgx",)], dma=True)
                    R.op("sp", lambda e: e.dma_start(out=dbg_h.rearrange("p (k n) -> p k n", k=KC), in_=hn), r=allk, w=[("dbgh",)], dma=True)
                    R.op("sp", lambda e: e.dma_start(out=dbg_a.rearrange("p (k n) -> p k n", k=32), in_=abuf), r=allk, w=[("dbga",)], dma=True)
                    continue
                R.op("sp", lambda e, t=t: e.dma_start(out=h1s[t].rearrange("p (k n) -> p k n", k=KC), in_=xh[:, :, 1:TW + 1]),
                     r=[("xh", kc) for kc in range(KC)], w=[("h1s", t)], dma=True)
                rmsnorm(2, False)
                rf = lambda gk: hn[:, gk, 1:TW + 1]
                rk = lambda gk: ("hn", gk)
                cnt = 0
                for h in range(NH):
                    for which in range(3):
                        ps, pk = ps_next()
                        if which < 2:
                            mm_group([P_QKV + 3 * h + which], rf, rk, ps, pk)
                            qb = qkvb[cnt % 3]
                            qk = ("qkvb", cnt % 3)
                            cnt += 1
                            R.op("act", lambda e, qb=qb, ps=ps: e.copy(out=qb, in_=ps),
                                 r=[pk, ("cols",)], w=[qk])
                            dst = (Qs, Ks)[which]
                            R.op("sp", lambda e, qb=qb, dst=dst, h=h, t=t: e.dma_start(
                                out=dst[h][:, t * TW:(t + 1) * TW], in_=qb),
                                r=[qk], w=[("QK", which, h, t)], dma=True)
                        else:
                            wt, wkey = w_next(P_QKV + 3 * h + 2)
                            for s_ in range(4):
                                for kc in range(KC):
                                    R.op("pe", lambda e, wt=wt, kc=kc, s_=s_, ps=ps: e.matmul(
                                        ps[:, s_ * 128:(s_ + 1) * 128], lhsT=hn[:, kc, 1 + s_ * 128:1 + (s_ + 1) * 128],
                                        rhs=wt[:, kc, :], start=(kc == 0), stop=(kc == KC - 1)),
                                        r=[wkey, ("hn", kc)], w=[pk])
                            vt = vtok[h % 2]
                            R.op("act", lambda e, vt=vt, ps=ps: e.copy(out=vt, in_=ps),
                                 r=[pk, ("cols",)], w=[("vtok", h % 2)])
                            R.op("sp", lambda e, vt=vt, h=h, t=t: e.dma_start(
                                out=Vs[h].rearrange("p (k n) -> p k n", k=KC)[:, t * 4:(t + 1) * 4, :],
                                in_=vt.rearrange("p (k n) -> p k n", k=4)),
                                r=[("vtok", h % 2)], w=[("V", h, t)], dma=True)

        if mode == "ALL":
            for h in range(NH):
                for kv in range(2):
                    dep = [("QK", 1, h, t) for t in range(NT)] if kv == 0 else [("V", h, t) for t in range(NT)]
                    R.op("pool", lambda e, h=h, kv=kv: e.collective_compute(
                        "AllGather", ALU.bypass, replica_groups=[[0, 1], [2, 3], [4, 5], [6, 7]],
                        ins=[KVs[h, kv]], outs=[KVf[h, kv]]), r=dep, w=[("KVf", h, kv)], dma=True, cc=True)

        if doB:
            bg_flush()
            pp_advance(NCH)
            R.op("sp", lambda e: e.dma_start(out=lamb.rearrange("p a n -> p (a n)"),
                                             in_=lamp.rearrange("a n -> (a n)").partition_broadcast(128)),
                 w=[("lamb",)], dma=True)
            R.op("dve", lambda e: e.tensor_tensor(out=lamt[:, 0, :], in0=lamb[:, 0, :], in1=lamb[:, 1, :], op=ALU.mult),
                 r=[("lamb",)], w=[("lamt", 0)])
            R.op("dve", lambda e: e.tensor_tensor(out=lamt[:, 1, :], in0=lamb[:, 2, :], in1=lamb[:, 3, :], op=ALU.mult),
                 r=[("lamb",)], w=[("lamt", 1)])
            R.op("dve", lambda e: e.reduce_sum(out=lams[:, 0:2], in_=lamt, axis=mybir.AxisListType.X),
                 r=[("lamt", 0), ("lamt", 1)], w=[("lams",)])
            R.op("act", lambda e: e.activation(out=lams[:, 2:4], in_=lams[:, 0:2], func=AF.Exp, bias=zero_c, scale=1.0),
                 r=[("lams",), ("cols",)], w=[("lams", 1)])
            R.op("dve", lambda e: e.scalar_tensor_tensor(out=nlam_c, in0=lams[:, 3:4], scalar=-LAMBDA_INIT1,
                                                         in1=lams[:, 2:3], op0=ALU.add, op1=ALU.subtract),
                 r=[("lams", 1), ("cols",)], w=[("cols", 3)])
            R.op("dve", lambda e: e.tensor_scalar(out=subg_c, in0=smb[:, 3 * KC:3 * KC + 1], scalar1=1.0 - LAMBDA_INIT1,
                                                  scalar2=None, op0=ALU.mult),
                 r=[("smb",), ("cols",)], w=[("cols", 4)])
            R.op("pool", lambda e: e.iota(Ti, [[1, TWID]], base=-TOFF, channel_multiplier=-1),
                 w=[("T",)])
            R.op("pool", lambda e: e.tensor_copy(out=Tt, in_=Ti), r=[("T",)], w=[("T",)])
            R.op("act", lambda e: e.activation(out=Tt, in_=Tt, func=AF.Abs, bias=smb[:, 3 * KC + 1:3 * KC + 2], scale=1.0),
                 r=[("T",), ("smb",)], w=[("T",)])

            def kv_src(h):
                if mode == "B":
                    return [Kf[0, h], Kf[1, h]], [Vf[0, h], Vf[1, h]]
                return [KVf[h, 0][0:128, :], KVf[h, 0][128:256, :]], [KVf[h, 1][0:128, :], KVf[h, 1][128:256, :]]

            def att_load(t, h, par):
                ks, vs = kv_src(h)
                dep = [("KVf", h, 0)] if mode == "ALL" else []
                depv = [("KVf", h, 1)] if mode == "ALL" else []
                for r_ in range(2):
                    R.op("sp", lambda e, r_=r_, ks=ks: e.dma_start(out=kTb[par][:, r_ * HALF:(r_ + 1) * HALF], in_=ks[r_]),
                         r=dep, w=[("kT", par, r_)], dma=True)
                    R.op("sp", lambda e, r_=r_, vs=vs: e.dma_start(
                        out=vhb[par][:, r_ * 16:(r_ + 1) * 16, :], in_=vs[r_].rearrange("p (k n) -> p k n", k=16)),
                        r=depv, w=[("vh", par, r_)], dma=True)
                for c in range(2):
                    R.op("sp", lambda e, c=c: e.dma_start(out=qTb[par][c][c * 64:(c + 1) * 64, :],
                                                          in_=Qs[h][c * 64:(c + 1) * 64, t * TW:(t + 1) * TW]),
                         r=[("QK", 0, h, t)], w=[("qT", par, c)], dma=True)

            for par_ in range(2):
                for c in range(2):
                    R.op("pool", lambda e, par_=par_, c=c: e.memset(qTb[par_][c], 0.0), w=[("qT", par_, c)])
            LA = 3
            NSC, NE = 3, 5
            SBANKS = (0, 1, 2, 7)
            gstep = [0]
            for t in range(NTL if DEBUG["Bstage"] >= 1 else 0):
                NHL = DEBUG["NHL"]
                att_load(t, 0, 0)
                steps = [(h, sub, kt) for h in range(NHL) for sub in range(2) for kt in range(32)]
                info = {}

                def s_stage(i, t=t):
                    h, sub, kt = steps[i]
                    par = h % 2
                    if sub == 0 and kt == LA and h + 1 < NHL:
                        att_load(t, h + 1, 1 - par)
                    slope8 = -8.0 * float(2.0 ** (-8.0 * (h + 1) / NH))
                    kT, qT = kTb[par], qTb[par]
                    qs = sub * 256
                    g = gstep[0]
                    gstep[0] += 1
                    sbi = SBANKS[g % 4]
                    Sb, Sk = banks[sbi], ("ps", sbi)
                    r_ = kt // 16
                    R.op("pe", lambda e, kt=kt, Sb=Sb, kT=kT, q2=qT2[par], qs=qs: e.matmul(
                        Sb.rearrange("p (c n) -> p c n", c=2), lhsT=kT[:, kt * 128:(kt + 1) * 128],
                        rhs=q2[:, :, qs:qs + 256], start=True, stop=True),
                        r=[("kT", par, r_), ("qT", par, 0), ("qT", par, 1)], w=[Sk])
                    x0 = t * TW + qs - kt * 128 + TOFF
                    sc, sck = scb[g % NSC], ("sc", g % NSC)
                    R.op("dve", lambda e, Sb=Sb, sc=sc, x0=x0, slope8=slope8: e.scalar_tensor_tensor(
                        out=sc.rearrange("p (c n) -> p c n", c=2),
                        in0=Tt[:, x0:x0 + 256].unsqueeze(1).to_broadcast([128, 2, 256]),
                        scalar=slope8, in1=Sb.rearrange("p (c n) -> p c n", c=2),
                        op0=ALU.mult, op1=ALU.add),
                        r=[Sk, ("T",)], w=[sck])
                    Et, Ek = Eb[g % NE], ("E", g % NE)
                    R.op("act", lambda e, sc=sc, Et=Et: e.activation(out=Et, in_=sc, func=AF.Exp,
                                                                     bias=shift_c, scale=0.125),
                         r=[sck, ("cols",)], w=[Ek])
                    info[i] = (Et, Ek)

                def av_stage(i, t=t):
                    h, sub, kt = steps[i]
                    par = h % 2
                    vh = vhb[par]
                    qs = sub * 256
                    oh = ohb[h % 2]
                    Et, Ek = info.pop(i)
                    Ob, Ok = (banks[3], ("ps", 3)) if sub == 0 else (banks[5], ("ps", 5))
                    Zb, Zk = (banks[4], ("ps", 4)) if sub == 0 else (banks[6], ("ps", 6))
                    R.op("pe", lambda e, kt=kt, Et=Et, Ob=Ob, vh=vh: e.matmul(
                        Ob, lhsT=vh[:, kt, :], rhs=Et, start=(kt == 0), stop=(kt == 31)),
                        r=[("vh", par, kt // 16), Ek], w=[Ok])
                    R.op("pe", lambda e, kt=kt, Et=Et, Zb=Zb: e.matmul(
                        Zb, lhsT=ones, rhs=Et, start=(kt == 0), stop=(kt == 31)),
                        r=[("ones",), Ek], w=[Zk])
                    if kt < 31:
                        return
                    def stA(Zb=Zb, Zk=Zk):
                        R.op("act", lambda e: e.activation(out=rzb, in_=Zb, func=AF.Ln, bias=zero_c, scale=1.0),
                             r=[Zk, ("cols",)], w=[("rz",)])

                    def stA2():
                        R.op("act", lambda e: e.activation(out=rzb, in_=rzb, func=AF.Exp, bias=zero_c, scale=-1.0),
                             r=[("rz",), ("cols",)], w=[("rz",)])

                    def stB(Ob=Ob, Ok=Ok):
                        R.op("dve", lambda e: e.tensor_tensor(out=tb, in0=Ob, in1=rzb, op=ALU.mult),
                             r=[Ok, ("rz",)], w=[("tb",)])

                    def stB2(oh=oh, qs=qs, h=h, sub=sub):
                        R.op("dve", lambda e: e.scalar_tensor_tensor(
                            out=oh[:, qs:qs + 256], in0=tb[:, 256:512], scalar=nlam_c, in1=tb[:, 0:256],
                            op0=ALU.mult, op1=ALU.add),
                            r=[("tb",), ("cols", 3)], w=[("oh", h % 2, sub)])
                    stages = [stA, stA2, stB, stB2]
                    if sub == 1:
                        ohk = [("oh", h % 2, 0), ("oh", h % 2, 1)]

                        def stC(oh=oh, ohk=ohk):
                            R.op("act", lambda e: e.activation(out=osq, in_=oh, func=AF.Square, bias=zero_c, scale=1.0),
                                 r=ohk + [("cols",)], w=[("osq",)])

                        def stD():
                            R.op("pe", lambda e: e.matmul(banks[7], lhsT=ones, rhs=osq, start=True, stop=True),
                                 r=[("ones",), ("osq",)], w=[("ps", 7)])

                        def stE():
                            R.op("act", lambda e: e.activation(out=rstd[:, 1:TW + 1], in_=banks[7], func=AF.Ln, bias=seps_c,
                                                               scale=1.0 / 128),
                                 r=[("ps", 7), ("cols", 2)], w=[("rstd",)])

                        def stE2():
                            R.op("act", lambda e: e.activation(out=rstd[:, 1:TW + 1], in_=rstd[:, 1:TW + 1], func=AF.Exp,
                                                               bias=zero_c, scale=-0.5),
                                 r=[("rstd",), ("cols",)], w=[("rstd",)])

                        def stF(oh=oh, h=h, ohk=ohk):
                            R.op("dve", lambda e: e.scalar_tensor_tensor(
                                out=hn[:, h, 1:TW + 1], in0=oh, scalar=subg_c, in1=rstd[:, 1:TW + 1], op0=ALU.mult, op1=ALU.mult),
                                r=ohk + [("cols", 4), ("rstd",)], w=[("hn", h)])
                        stages += [stC, (lambda: (stD(), stE())), stE2, stF]
                    for n_, f_ in enumerate(stages):
                        pending.append((i + 2 + 2 * n_, f_))

                pending = []
                for i in range(len(steps) + LA):
                    if i < len(steps):
                        s_stage(i)
                    if i - LA >= 0:
                        av_stage(i - LA)
                        while pending and pending[0][0] <= i - LA:
                            pending.pop(0)[1]()
                while pending:
                    pending.pop(0)[1]()
                if DEBUG["Bstage"] < 2:
                    R.op("sp", lambda e, t=t: e.dma_start(out=outT[t].rearrange("p (k n) -> p k n", k=KC)[:, :, 0:257], in_=hn[:, :, 0:514].bitcast(F32)),
                         r=[("hn", kc) for kc in range(KC)], w=[("out", t)], dma=True)
                    continue
                R.op("sp", lambda e, t=t: e.dma_start(out=xh[:, :, 1:TW + 1], in_=h1s[t].rearrange("p (k n) -> p k n", k=KC)),
                     r=[("h1s", t)], w=[("xh", kc) for kc in range(KC)], dma=True)
                resid_linear(P_WO, 1, lambda gk: hn[:, gk, 1:TW + 1], lambda gk: ("hn", gk))
                rmsnorm(3, False)
                mlp(P_MLP1)
                rmsnorm(4, False)
                for kc in range(KC):
                    R.op("dve", lambda e, kc=kc: e.scalar_tensor_tensor(
                        out=xh[:, kc, 1:TW + 1], in0=xh[:, kc, 1:TW + 1], scalar=gsb[:, 4 * KC + kc:4 * KC + kc + 1],
                        in1=rstd[:, 1:TW + 1], op0=ALU.mult, op1=ALU.mult),
                        r=[("xh", kc), ("gsb",), ("rstd",), ("hn", kc)], w=[("xh", kc)])
                R.op("sp", lambda e, t=t: e.dma_start(out=outT[t].rearrange("p (k n) -> p k n", k=KC), in_=xh[:, :, 1:TW + 1]),
                     r=[("xh", kc) for kc in range(KC)], w=[("out", t)], dma=True)

        pp_advance(NCH)
        assert wst["use"] == len(seq), (wst, len(seq))
        fin = Item("sp", None, set(R.dma_last.values()), False)
        for d in fin.deps:
            d.needed = True
        R.q["sp"].append(fin)
        with nc.Block() as block:
            R.emit(nc, block, semh)
        print("SEMSTAT", mode, {e: (len(R.q[e]), max([it.val for it in R.q[e] if not it.is_dma] + [0])) for e in Rec.ENGS},
              "dma", max(R.dma_cnt.values()), "sbuf", off[0] * 2)
    return nc


def _panels(W, r0s, c0s):
    out = np.empty((len(r0s), 128, 2048), np.float32)
    for i, (r0, c0) in enumerate(zip(r0s, c0s)):
        out[i] = W[r0:r0 + 2048, c0:c0 + 128].reshape(16, 128, 128).transpose(1, 0, 2).reshape(128, 2048)
    return out


def _mlp_panels(w1, w2):
    ps = []
    for half in range(2):
        cs = [(half * 32 + fo) * 128 for fo in range(32)]
        ps.append(_panels(w1, [0] * 32, cs))
        r0s, c0s = [], []
        for do in range(16):
            for s in range(2):
                r0s.append(half * 4096 + s * 2048)
                c0s.append(do * 128)
        ps.append(_panels(w2, r0s, c0s))
    return np.concatenate(ps, 0)


def _layout_inputs(inp):
    x = np.asarray(inp["x"], np.float32)
    w_in = np.asarray(inp["conv_w_in"][0], np.float32)
    cs = []
    for j in range(16):
        cs += [(16 + j) * 128, (32 + j) * 128, j * 128]
    pan = [_panels(w_in, [0] * 48, cs)]
    pan.append(_panels(np.asarray(inp["conv_w_out"][0], np.float32), [0] * 16, [d * 128 for d in range(16)]))
    pan.append(_mlp_panels(np.asarray(inp["mlp_w1"][0], np.float32), np.asarray(inp["mlp_w2"][0], np.float32)))
    cs = []
    for h in range(16):
        cs += [h * 128, (16 + h) * 128, (32 + h) * 128]
    pan.append(_panels(np.asarray(inp["attn_w_qkv"][0], np.float32), [0] * 48, cs))
    pan.append(_panels(np.asarray(inp["attn_w_o"][0], np.float32), [0] * 16, [d * 128 for d in range(16)]))
    pan.append(_mlp_panels(np.asarray(inp["mlp_w1"][1], np.float32), np.asarray(inp["mlp_w2"][1], np.float32)))
    wpan = np.concatenate(pan, 0)
    assert wpan.shape[0] == NPAN
    fm = lambda v: np.asarray(v, np.float32).reshape(16, 128).T
    gains = np.concatenate([fm(inp["ln_mix"][0]), fm(inp["ln_mlp"][0]), fm(inp["ln_mix"][1]),
                            fm(inp["ln_mlp"][1]), fm(inp["ln_f"])], 1)
    cw = np.asarray(inp["conv_w"][0], np.float32)
    convw = np.concatenate([fm(cw[0]), fm(cw[1]), fm(cw[2])], 1)
    subln = np.asarray(inp["attn_subln"][0], np.float32).reshape(128, 1)
    lamp = np.stack([np.asarray(inp[k][0], np.float32) for k in
                     ("attn_lambda_q1", "attn_lambda_k1", "attn_lambda_q2", "attn_lambda_k2")], 0)
    per_core = []
    xp = np.zeros((4, SEQ + 2, D), np.float32)
    xp[:, 1:SEQ + 1] = x
    for c in range(NCORE):
        b, hf = c // 2, c % 2
        xt = np.empty((NT, 128, KC, TWH), np.float32)
        for t in range(NT):
            s0 = hf * HALF + t * TW
            blk = xp[b, s0:s0 + TWH]
            xt[t] = blk.T.reshape(KC, 128, TWH).transpose(1, 0, 2)
        smalls = np.concatenate([convw, subln, np.full((128, 1), hf * HALF, np.float32)], 1)
        per_core.append({"xT": xt.reshape(NT, 128, KC * TWH), "smalls": np.ascontiguousarray(smalls)})
    return wpan, np.ascontiguousarray(gains), np.ascontiguousarray(lamp), per_core


def _assemble(outs):
    out = np.empty((4, SEQ, D), np.float32)
    for c in range(NCORE):
        b, hf = c // 2, c % 2
        o = outs[c].reshape(NT, 128, KC, TW)
        for t in range(NT):
            s0 = hf * HALF + t * TW
            out[b, s0:s0 + TW] = o[t].transpose(2, 1, 0).reshape(TW, D)
    return out


FUSED = True
DEBUG = {"NT": NT, "stop": 99, "Bstage": 2, "NHL": NH, "v": 0}
_cache = {}


def _get(mode):
    if mode not in _cache:
        _cache[mode] = build(mode)
    return _cache[mode]


def kernel(**inp):
    wpan, gains, lamp, pc = _layout_inputs(inp)
    cores = list(range(NCORE))
    if FUSED:
        nc = _get("ALL")
        maps = [{"wpan": wpan, "gains": gains, "lamp": lamp, "smalls": pc[c]["smalls"], "xT": pc[c]["xT"]}
                for c in cores]
        res = run_bass_kernel_spmd(nc, maps, core_ids=cores)
        return _assemble([np.asarray(r["outT"]) for r in res.results])
    ncA = _get("A")
    wA = np.ascontiguousarray(wpan[:P_WO])
    maps = [{"wpan": wA, "gains": gains, "lamp": lamp, "smalls": pc[c]["smalls"], "xT": pc[c]["xT"]} for c in cores]
    ra = run_bass_kernel_spmd(ncA, maps, core_ids=cores).results
    ncB = _get("B")
    wB = np.ascontiguousarray(wpan[P_WO:])
    maps = []
    for c in cores:
        c0 = (c // 2) * 2
        Kf = np.stack([np.asarray(ra[c0]["Ks"]), np.asarray(ra[c0 + 1]["Ks"])], 0)
        Vf = np.stack([np.asarray(ra[c0]["Vs"]), np.asarray(ra[c0 + 1]["Vs"])], 0)
        maps.append({"wpan": wB, "gains": gains, "lamp": lamp, "smalls": pc[c]["smalls"],
                     "h1s": np.asarray(ra[c]["h1s"]), "Qs": np.asarray(ra[c]["Qs"]), "Kf": Kf, "Vf": Vf})
    rb = run_bass_kernel_spmd(ncB, maps, core_ids=cores).results
    return _assemble([np.asarray(r["outT"]) for r in rb])
```

```python
import numpy as np
from contextlib import ExitStack
import concourse.bass as bass
import concourse.mybir as mybir
from concourse.bass_utils import run_bass_kernel_spmd

F32 = mybir.dt.float32
BF16 = mybir.dt.bfloat16
I32 = mybir.dt.int32
ALU = mybir.AluOpType
AF = mybir.ActivationFunctionType

D = 2048
KC = 16
NT = 4
TW = 512
TWH = 514
NCORE = 8
SEQ = 4096
HALF = 2048
NH = 16
EPS = 1e-6
SUBEPS = 1e-5
LAMBDA_INIT1 = 0.8 - 0.6 * float(np.exp(-0.3 * 1))
TOFF = 3968
TWID = 6016
NDMASEM = 8

P_WIN, P_WOUT, P_MLP0, P_QKV, P_WO, P_MLP1 = 0, 48, 64, 192, 240, 256
NPAN = 384


class Item:
    __slots__ = ("eng", "fn", "deps", "needed", "val", "sem", "is_dma", "inc")

    def __init__(self, eng, fn, deps, is_dma):
        self.eng, self.fn, self.deps, self.is_dma = eng, fn, deps, is_dma
        self.needed = False
        self.val = 0
        self.sem = None
        self.inc = 16


class Rec:
    ENGS = ("sp", "act", "pe", "dve", "pool")

    def __init__(self):
        self.q = {e: [] for e in self.ENGS}
        self.lastw = {}
        self.reads = {}
        self.dma_rot = {e: 0 for e in self.ENGS}
        self.dma_last = {}
        self.dma_cnt = {}

    def op(self, eng, fn, r=(), w=(), dma=False, cc=False, after=()):
        deps = set()
        for k in after:
            t = self.lastw.get(k)
            if t is not None:
                deps.add(t)
        for k in r:
            t = self.lastw.get(k)
            if t is not None:
                deps.add(t)
        for k in w:
            t = self.lastw.get(k)
            if t is not None:
                deps.add(t)
            for t in self.reads.get(k, ()):
                deps.add(t)
        if eng == "pe" and not dma:
            deps = {d for d in deps if d.is_dma or d.eng != "pe"}
        it = Item(eng, fn, deps, dma)
        if cc:
            it.sem = ("cc", 0)
            self.dma_cnt[it.sem] = self.dma_cnt.get(it.sem, 0) + 1
            it.val = self.dma_cnt[it.sem]
            it.inc = 1
            self.dma_last[it.sem] = it
        elif dma:
            j = self.dma_rot[eng] % NDMASEM
            self.dma_rot[eng] += 1
            sem = (eng, j)
            prev = self.dma_last.get(sem)
            if prev is not None:
                deps.add(prev)
            self.dma_last[sem] = it
            self.dma_cnt[sem] = self.dma_cnt.get(sem, 0) + 16
            it.sem = sem
            it.val = self.dma_cnt[sem]
        deps.discard(it)
        for d in deps:
            d.needed = True
        self.q[eng].append(it)
        for k in r:
            lst = self.reads.setdefault(k, [])
            if not dma:
                for i_, o in enumerate(lst):
                    if o.eng == eng and not o.is_dma:
                        lst[i_] = it
                        break
                else:
                    lst.append(it)
            else:
                lst.append(it)
        for k in w:
            self.lastw[k] = it
            self.reads[k] = []
        return it

    def emit(self, nc, block, semh):
        for eng in self.ENGS:
            cnt = 0
            for it in self.q[eng]:
                if not it.is_dma and it.needed:
                    cnt += 1
                    it.sem = eng
                    it.val = cnt
        bname = {"sp": "sync", "act": "scalar", "pe": "tensor", "dve": "vector", "pool": "gpsimd"}
        for eng in self.ENGS:
            items = self.q[eng]

            def body(e, items=items, eng=eng):
                known = {}
                for it in items:
                    waits = {}
                    for d in it.deps:
                        if eng == "pe" and d.eng == "pe" and not d.is_dma:
                            continue
                        if waits.get(d.sem, 0) < d.val:
                            waits[d.sem] = d.val
                    for s, v in waits.items():
                        if known.get(s, 0) >= v:
                            continue
                        known[s] = v
                        e.wait_ge(semh[s], v)
                    if it.fn is None:
                        continue
                    ins = it.fn(e)
                    if it.is_dma:
                        ins.then_inc(semh[it.sem], it.inc)
                    elif it.needed:
                        ins.then_inc(semh[eng], 1)

            getattr(block, bname[eng])(body)


def build(mode):
    doA = mode in ("A", "ALL")
    doB = mode in ("B", "ALL")
    NTL = DEBUG["NT"]
    STOP = DEBUG["stop"]
    nc = bass.Bass("TRN2", target_bir_lowering=False)
    R = Rec()

    def dt_in(name, shape, dt=F32):
        return nc.dram_tensor(name, list(shape), dt, kind="ExternalInput").ap()

    def dt_out(name, shape, dt=F32):
        return nc.dram_tensor(name, list(shape), dt, kind="ExternalOutput").ap()

    def dt_int(name, shape, dt=F32):
        return nc.dram_tensor(name, list(shape), dt).ap()

    p_lo = 0 if doA else P_WO
    p_hi = NPAN if doB else P_WO
    npan = p_hi - p_lo
    wpan = dt_in("wpan", [npan, 128, 2048])
    wbf = dt_int("wbf", [npan, 128, 2048], BF16)
    gains = dt_in("gains", [128, 5 * KC])
    smalls = dt_in("smalls", [128, 3 * KC + 1 + 1])
    if doB:
        lamp = dt_in("lamp", [4, 64])
    if doA:
        xT = dt_in("xT", [NT, 128, KC * TWH])
    if mode == "A":
        h1s = dt_out("h1s", [NT, 128, KC * TW])
        Qs = dt_out("Qs", [NH, 128, HALF], BF16)
        Ks = dt_out("Ks", [NH, 128, HALF], BF16)
        Vs = dt_out("Vs", [NH, 128, KC * 128], BF16)
    elif mode == "B":
        h1s = dt_in("h1s", [NT, 128, KC * TW])
        Qs = dt_in("Qs", [NH, 128, HALF], BF16)
        Kf = dt_in("Kf", [2, NH, 128, HALF], BF16)
        Vf = dt_in("Vf", [2, NH, 128, KC * 128], BF16)
    else:
        h1s = dt_int("h1s", [NT, 128, KC * TW])
        Qs = dt_int("Qs", [NH, 128, HALF], BF16)
        KVs = dt_int("KVs", [NH, 2, 128, HALF], BF16)
        KVf = dt_int("KVf", [NH, 2, 2 * 128, HALF], BF16)
        Ks = [KVs[h, 0] for h in range(NH)]
        Vs = [KVs[h, 1] for h in range(NH)]
    if doB:
        outT = dt_out("outT", [NT, 128, KC * TW])

    es = ExitStack()
    with es:
        ARENA = 212480 // 2
        arena = es.enter_context(nc.sbuf_tensor("arena", [128, ARENA], BF16))
        off = [0]

        def carve(nelem, dt):
            nb = nelem * (4 if dt in (F32, I32) else 2)
            nb = (nb + 63) // 64 * 64
            o = off[0]
            off[0] += nb // 2
            assert off[0] <= ARENA, ("sbuf overflow", off[0] * 2)
            v = arena[:, o:o + nb // 2]
            if dt != BF16:
                v = v.bitcast(dt)
            return v[:, 0:nelem]

        xh = carve(KC * TWH, F32).rearrange("p (k n) -> p k n", k=KC)
        hn = carve(KC * TWH, BF16).rearrange("p (k n) -> p k n", k=KC)
        abuf = carve(32 * TWH, BF16).rearrange("p (k n) -> p k n", k=32)
        wring = [carve(2048, BF16).rearrange("p (k n) -> p k n", k=KC) for _ in range(4)]
        st_o = off[0]
        Tt = carve(TWID, F32)
        Ti = arena[:, st_o:st_o + 2 * TWID].bitcast(I32)
        st32 = [arena[:, st_o + i * 2048:st_o + (i + 1) * 2048].bitcast(F32) for i in range(2)]
        st16 = [arena[:, st_o + 4096 + i * 1024:st_o + 4096 + (i + 1) * 1024] for i in range(2)]
        rstd = carve(TWH, F32)
        csb = [carve(TWH, F32) for _ in range(2)]
        upb = [carve(TWH, F32) for _ in range(2)]
        accb = [carve(TW, F32) for _ in range(2)]
        rbuf = [carve(TW, F32) for _ in range(2)]
        qkvb = [carve(TW, BF16) for _ in range(3)]
        vtok = [carve(TW, BF16) for _ in range(2)]
        kTb = [carve(SEQ, BF16) for _ in range(2)]
        vhb = [carve(SEQ, BF16).rearrange("p (k n) -> p k n", k=32) for _ in range(2)]
        qTo = [off[0], off[0] + 2 * TW]
        qTb = [[carve(TW, BF16) for _ in range(2)] for _ in range(2)]
        qT2 = [arena[:, qTo[p_]:qTo[p_] + 2 * TW].rearrange("p (c n) -> p c n", c=2) for p_ in range(2)]
        scb = [carve(TW, F32) for _ in range(3)]
        Eb = [carve(TW, BF16) for _ in range(5)]
        rzb = carve(TW, F32)
        tb = carve(TW, F32)
        ohb = [carve(TW, F32) for _ in range(2)]
        osq = carve(TW, BF16)
        ones = carve(128, BF16)
        ident = carve(128, BF16)
        gsb = carve(5 * KC, F32)
        smb = carve(3 * KC + 2, F32)
        lamb = carve(4 * 64, F32).rearrange("p (a n) -> p a n", a=4)
        lamt = carve(2 * 64, F32).rearrange("p (a n) -> p a n", a=2)
        lams = carve(4, F32)
        cols = carve(8, F32)
        iot = tb.bitcast(I32)[:, 0:128]
        iotf = rzb[:, 0:128]
        ST_KEYS = [("st32", 0), ("st32", 1), ("st16", 0), ("st16", 1)]

        banks = [es.enter_context(nc.psum_tensor(f"bank{i}", [128, TW], F32))[:] for i in range(8)]
        semh = {}
        for e in Rec.ENGS:
            semh[e] = es.enter_context(nc.semaphore(f"s_{e}"))
        for e in ("sp", "pool", "act"):
            for j in range(NDMASEM):
                semh[(e, j)] = es.enter_context(nc.semaphore(f"d_{e}{j}"))

        semh[("cc", 0)] = es.enter_context(nc.semaphore("s_cc"))
        zero_c, eps_c, seps_c, nlam_c, subg_c, shift_c = (cols[:, i:i + 1] for i in range(6))

        R.op("pool", lambda e: e.memset(ones, 1.0), w=[("ones",)])
        R.op("pool", lambda e: e.memset(cols, 0.0), w=[("cols",)])
        R.op("pool", lambda e: e.memset(eps_c, EPS), r=[("cols",)], w=[("cols", 1)])
        R.op("pool", lambda e: e.memset(seps_c, SUBEPS), r=[("cols",)], w=[("cols", 2)])
        R.op("pool", lambda e: e.iota(iot, [[1, 128]], base=0, channel_multiplier=-1), w=[("tb",)])
        R.op("pool", lambda e: e.tensor_copy(out=iotf, in_=iot), r=[("tb",)], w=[("rz",)])
        R.op("pool", lambda e: e.tensor_scalar(out=ident, in0=iotf, scalar1=0.0, scalar2=None, op0=ALU.is_equal),
             r=[("rz",)], w=[("ident",)])
        R.op("sp", lambda e: e.dma_start(out=gsb, in_=gains), w=[("gsb",)], dma=True)
        R.op("sp", lambda e: e.dma_start(out=smb, in_=smalls), w=[("smb",)], dma=True)
        CONST_KEYS = [("cols",), ("cols", 1), ("cols", 2), ("gsb",), ("smb",), ("ones",), ("ident",)]

        NCH = npan

        def pp_advance(upto):
            pass

        n_direct = (P_WO - p_lo) if doA else 0
        bg_state = {"done": False}

        bg_state["next"] = n_direct
        bg_state["armed"] = False
        BG_EVERY = 4

        def bg_one(pace_key=None):
            pid_ = bg_state["next"]
            if pid_ >= npan:
                return
            bg_state["next"] += 1
            R.op("pool", lambda e, pid_=pid_: e.dma_start(out=wbf[pid_], in_=wpan[pid_]),
                 after=([pace_key] if pace_key is not None else []), w=[("wbf", pid_)], dma=True)

        def emit_background():
            bg_state["armed"] = True

        def bg_flush():
            while bg_state["next"] < npan:
                bg_one()

        if n_direct == 0:
            bg_flush()
        direct_done = set()

        seq = []
        if doA:
            for t in range(NTL):
                if STOP >= 2:
                    seq += list(range(P_WIN, P_WIN + 48))
                if STOP >= 3:
                    seq += list(range(P_WOUT, P_WOUT + 16))
                if STOP >= 4:
                    seq += list(range(P_MLP0, P_MLP0 + 128))
                if STOP >= 5:
                    seq += list(range(P_QKV, P_QKV + 48))
        if doB and DEBUG["Bstage"] >= 2:
            for t in range(NTL):
                seq += list(range(P_WO, P_WO + 16)) + list(range(P_MLP1, P_MLP1 + 128))
        seq = [p - p_lo for p in seq]
        wst = {"ld": 0, "use": 0}

        def w_prefetch(upto):
            upto = min(upto, len(seq))
            while wst["ld"] < upto:
                i = wst["ld"]
                pid = seq[i]
                if pid < n_direct and pid not in direct_done:
                    direct_done.add(pid)
                    R.op("pool", lambda e, i=i, pid=pid: e.dma_start(
                        out=wring[i % 4], in_=wpan[pid].rearrange("p (k n) -> p k n", k=KC)),
                        w=[("w", i % 4)], dma=True)
                    R.op("sp", lambda e, i=i, pid=pid: e.dma_start(
                        out=wbf[pid].rearrange("p (k n) -> p k n", k=KC), in_=wring[i % 4]),
                        r=[("w", i % 4)], w=[("wbf", pid)], dma=True)
                    if len(direct_done) == n_direct:
                        emit_background()
                else:
                    R.op("sp", lambda e, i=i, pid=pid: e.dma_start(
                        out=wring[i % 4], in_=wbf[pid].rearrange("p (k n) -> p k n", k=KC)),
                        r=[("wbf", pid)], w=[("w", i % 4)], dma=True)
                wst["ld"] += 1

        def w_next(expect):
            i = wst["use"]
            assert seq[i] == expect - p_lo, (i, seq[i], expect)
            w_prefetch(i + 4)
            wst["use"] += 1
            if bg_state["armed"] and i % BG_EVERY == 0:
                bg_one(("w", i % 4))
            return wring[i % 4], ("w", i % 4)

        psrot = [0]

        def ps_next():
            b = psrot[0] % 5
            psrot[0] += 1
            return banks[b], ("ps", b)

        def mm_group(pan_ids, rhs_fn, rhs_keys, ps, pskey, extra=None):
            n = len(pan_ids) * KC
            idx = 0
            for pi, pid in enumerate(pan_ids):
                wt, wkey = w_next(pid)
                for kc in range(KC):
                    gk = pi * KC + kc
                    R.op("pe", lambda e, wt=wt, kc=kc, gk=gk, idx=idx: e.matmul(
                        ps, lhsT=wt[:, kc, :], rhs=rhs_fn(gk), start=(idx == 0), stop=(idx == n - 1)),
                        r=[wkey, rhs_keys(gk)], w=[pskey])
                    if extra is not None:
                        extra(wt, wkey, kc, gk, idx, n)
                    idx += 1

        def rmsnorm(gi, halo, src_keys=None):
            c0, c1 = (0, TWH) if halo else (1, TW + 1)
            for kc in range(KC):
                R.op("act", lambda e, kc=kc: e.activation(out=abuf[:, 16 + kc, c0:c1], in_=xh[:, kc, c0:c1],
                                                          func=AF.Square, bias=zero_c, scale=1.0),
                     r=[("xh", kc), ("cols",)], w=[("a", 16 + kc)])
            for kc in range(KC):
                R.op("pe", lambda e, kc=kc: e.matmul(banks[6], lhsT=ones, rhs=abuf[:, 16 + kc, 1:TW + 1],
                                                     start=(kc == 0), stop=(kc == KC - 1)),
                     r=[("ones",), ("a", 16 + kc)], w=[("ps", 6)])
                if halo:
                    R.op("pe", lambda e, kc=kc: e.matmul(banks[7][:, 0:2], lhsT=ones, rhs=abuf[:, 16 + kc, 0:TWH:TW + 1],
                                                         start=(kc == 0), stop=(kc == KC - 1)),
                         r=[("ones",), ("a", 16 + kc)], w=[("ps", 7)])
            R.op("act", lambda e: e.activation(out=rstd[:, 1:TW + 1], in_=banks[6], func=AF.Ln, bias=eps_c, scale=1.0 / D),
                 r=[("ps", 6), ("cols", 1)], w=[("rstd",)])
            if halo:
                R.op("act", lambda e: e.activation(out=rstd[:, 0:TWH:TW + 1], in_=banks[7][:, 0:2], func=AF.Ln,
                                                   bias=eps_c, scale=1.0 / D),
                     r=[("ps", 7), ("cols", 1)], w=[("rstd",)])
            R.op("act", lambda e: e.activation(out=rstd[:, c0:c1], in_=rstd[:, c0:c1], func=AF.Exp, bias=zero_c, scale=-0.5),
                 r=[("rstd",), ("cols",)], w=[("rstd",)])
            for kc in range(KC):
                R.op("dve", lambda e, kc=kc: e.scalar_tensor_tensor(
                    out=hn[:, kc, c0:c1], in0=xh[:, kc, c0:c1], scalar=gsb[:, gi * KC + kc:gi * KC + kc + 1],
                    in1=rstd[:, c0:c1], op0=ALU.mult, op1=ALU.mult),
                    r=[("xh", kc), ("gsb",), ("rstd",)], w=[("hn", kc)])

        def resid_linear(pbase, npan_per, rhs_fn, rhs_keys):
            for do in range(KC):
                ps, pk = ps_next()
                mm_group([pbase + do * npan_per + s for s in range(npan_per)], rhs_fn, rhs_keys, ps, pk)
                R.op("dve", lambda e, do=do, ps=ps: e.tensor_tensor(out=xh[:, do, 1:TW + 1], in0=ps,
                                                                    in1=xh[:, do, 1:TW + 1], op=ALU.add),
                     r=[pk, ("xh", do)], w=[("xh", do)])

        def mlp(pbase):
            for half in range(2):
                hb = pbase + half * 64
                for fo in range(32):
                    ps, pk = ps_next()
                    mm_group([hb + fo], lambda gk: hn[:, gk, 1:TW + 1], lambda gk: ("hn", gk), ps, pk)
                    rb = rbuf[fo % 2]
                    R.op("act", lambda e, ps=ps, rb=rb: e.activation(out=rb, in_=ps, func=AF.Relu, bias=zero_c, scale=1.0),
                         r=[pk, ("cols",)], w=[("rbuf", fo % 2)])
                    R.op("dve", lambda e, fo=fo, rb=rb: e.tensor_tensor(out=abuf[:, fo, 0:TW], in0=rb, in1=rb, op=ALU.mult),
                         r=[("rbuf", fo % 2)], w=[("a", fo)])
                resid_linear(hb + 32, 2, lambda gk: abuf[:, gk, 0:TW], lambda gk: ("a", gk))

        if doA:
            cw = smb
            if STOP < 99:
                dbg_x = dt_out("dbg_x", [128, KC * TWH])
                dbg_h = dt_out("dbg_h", [128, KC * TWH], BF16)
                dbg_a = dt_out("dbg_a", [128, 32 * TWH], BF16)
            for t in range(NTL):
                R.op("sp", lambda e, t=t: e.dma_start(out=xh, in_=xT[t].rearrange("p (k n) -> p k n", k=KC)),
                     w=[("xh", kc) for kc in range(KC)], dma=True)
                if STOP >= 1:
                    rmsnorm(0, True)
                for j in range(KC if STOP >= 2 else 0):
                    ps_c, pk_c = ps_next()
                    ps_u, pk_u = ps_next()
                    ps_b, pk_b = ps_next()
                    cs, up, acc = csb[j % 2], upb[j % 2], accb[j % 2]

                    def halo_mm(hb):
                        def f(wt, wkey, kc, gk, idx, n):
                            R.op("pe", lambda e: e.matmul(banks[hb][:, 0:2], lhsT=wt[:, kc, :],
                                                          rhs=hn[:, kc, 0:TWH:TW + 1], start=(idx == 0), stop=(idx == n - 1)),
                                 r=[wkey, ("hn", kc)], w=[("ps", hb)])
                        return f
                    rf = lambda gk: hn[:, gk, 1:TW + 1]
                    rk = lambda gk: ("hn", gk)
                    mm_group([P_WIN + 3 * j], rf, rk, ps_c, pk_c, extra=halo_mm(7))
                    R.op("act", lambda e, cs=cs, ps_c=ps_c: e.copy(out=cs[:, 1:TW + 1], in_=ps_c),
                         r=[pk_c, ("cols",)], w=[("cs", j % 2)])
                    R.op("act", lambda e, cs=cs: e.copy(out=cs[:, 0:TWH:TW + 1], in_=banks[7][:, 0:2]),
                         r=[("ps", 7), ("cols",)], w=[("csh", j % 2)])
                    mm_group([P_WIN + 3 * j + 1], rf, rk, ps_u, pk_u, extra=halo_mm(5))
                    R.op("dve", lambda e, cs=cs, up=up, ps_u=ps_u: e.tensor_tensor(
                        out=up[:, 1:TW + 1], in0=ps_u, in1=cs[:, 1:TW + 1], op=ALU.mult),
                        r=[pk_u, ("cs", j % 2)], w=[("up", j % 2)])
                    R.op("dve", lambda e, cs=cs, up=up: e.tensor_tensor(
                        out=up[:, 0:TWH:TW + 1], in0=banks[5][:, 0:2], in1=cs[:, 0:TWH:TW + 1], op=ALU.mult),
                        r=[("ps", 5), ("csh", j % 2)], w=[("uph", j % 2)])
                    R.op("dve", lambda e, up=up, acc=acc, j=j: e.tensor_scalar(
                        out=acc, in0=up[:, 1:TW + 1], scalar1=cw[:, KC + j:KC + j + 1], scalar2=None, op0=ALU.mult),
                        r=[("up", j % 2), ("smb",)], w=[("acc", j % 2)])
                    R.op("dve", lambda e, up=up, acc=acc, j=j: e.scalar_tensor_tensor(
                        out=acc, in0=up[:, 0:TW], scalar=cw[:, j:j + 1], in1=acc, op0=ALU.mult, op1=ALU.add),
                        r=[("up", j % 2), ("uph", j % 2), ("smb",), ("acc", j % 2)], w=[("acc", j % 2)])
                    R.op("dve", lambda e, up=up, acc=acc, j=j: e.scalar_tensor_tensor(
                        out=acc, in0=up[:, 2:TW + 2], scalar=cw[:, 2 * KC + j:2 * KC + j + 1], in1=acc,
                        op0=ALU.mult, op1=ALU.add),
                        r=[("up", j % 2), ("uph", j % 2), ("smb",), ("acc", j % 2)], w=[("acc", j % 2)])
                    mm_group([P_WIN + 3 * j + 2], rf, rk, ps_b, pk_b)
                    R.op("dve", lambda e, acc=acc, ps_b=ps_b, j=j: e.tensor_tensor(
                        out=abuf[:, j, 0:TW], in0=ps_b, in1=acc, op=ALU.mult),
                        r=[pk_b, ("acc", j % 2)], w=[("a", j)])
                if STOP >= 3:
                    resid_linear(P_WOUT, 1, lambda gk: abuf[:, gk, 0:TW], lambda gk: ("a", gk))
                if STOP >= 4:
                    rmsnorm(1, False)
                    mlp(P_MLP0)
                if STOP < 5:
                    allk = [("xh", kc) for kc in range(KC)] + [("hn", kc) for kc in range(KC)] + [("a", kc) for kc in range(32)]
                    R.op("sp", lambda e: e.dma_start(out=dbg_x.rearrange("p (k n) -> p k n", k=KC), in_=xh), r=allk, w=[("dbgx",)], dma=True)
                    R.op("sp", lambda e: e.dma_start(out=dbg_h.rearrange("p (k n) -> p k n", k=KC), in_=hn), r=allk, w=[("dbgh",)], dma=True)
                    R.op("sp", lambda e: e.dma_start(out=dbg_a.rearrange("p (k n) -> p k n", k=32), in_=abuf), r=allk, w=[("dbga",)], dma=True)
                    continue
                R.op("sp", lambda e, t=t: e.dma_start(out=h1s[t].rearrange("p (k n) -> p k n", k=KC), in_=xh[:, :, 1:TW + 1]),
                     r=[("xh", kc) for kc in range(KC)], w=[("h1s", t)], dma=True)
                rmsnorm(2, False)
                rf = lambda gk: hn[:, gk, 1:TW + 1]
                rk = lambda gk: ("hn", gk)
                cnt = 0
                for h in range(NH):
                    for which in range(3):
                        ps, pk = ps_next()
                        if which < 2:
                            mm_group([P_QKV + 3 * h + which], rf, rk, ps, pk)
                            qb = qkvb[cnt % 3]
                            qk = ("qkvb", cnt % 3)
                            cnt += 1
                            R.op("act", lambda e, qb=qb, ps=ps: e.copy(out=qb, in_=ps),
                                 r=[pk, ("cols",)], w=[qk])
                            dst = (Qs, Ks)[which]
                            R.op("sp", lambda e, qb=qb, dst=dst, h=h, t=t: e.dma_start(
                                out=dst[h][:, t * TW:(t + 1) * TW], in_=qb),
                                r=[qk], w=[("QK", which, h, t)], dma=True)
                        else:
                            wt, wkey = w_next(P_QKV + 3 * h + 2)
                            for s_ in range(4):
                                for kc in range(KC):
                                    R.op("pe", lambda e, wt=wt, kc=kc, s_=s_, ps=ps: e.matmul(
                                        ps[:, s_ * 128:(s_ + 1) * 128], lhsT=hn[:, kc, 1 + s_ * 128:1 + (s_ + 1) * 128],
                                        rhs=wt[:, kc, :], start=(kc == 0), stop=(kc == KC - 1)),
                                        r=[wkey, ("hn", kc)], w=[pk])
                            vt = vtok[h % 2]
                            R.op("act", lambda e, vt=vt, ps=ps: e.copy(out=vt, in_=ps),
                                 r=[pk, ("cols",)], w=[("vtok", h % 2)])
                            R.op("sp", lambda e, vt=vt, h=h, t=t: e.dma_start(
                                out=Vs[h].rearrange("p (k n) -> p k n", k=KC)[:, t * 4:(t + 1) * 4, :],
                                in_=vt.rearrange("p (k n) -> p k n", k=4)),
                                r=[("vtok", h % 2)], w=[("V", h, t)], dma=True)

        if mode == "ALL":
            for h in range(NH):
                for kv in range(2):
                    dep = [("QK", 1, h, t) for t in range(NT)] if kv == 0 else [("V", h, t) for t in range(NT)]
                    R.op("pool", lambda e, h=h, kv=kv: e.collective_compute(
                        "AllGather", ALU.bypass, replica_groups=[[0, 1], [2, 3], [4, 5], [6, 7]],
                        ins=[KVs[h, kv]], outs=[KVf[h, kv]]), r=dep, w=[("KVf", h, kv)], dma=True, cc=True)

        if doB:
            bg_flush()
            pp_advance(NCH)
            R.op("sp", lambda e: e.dma_start(out=lamb.rearrange("p a n -> p (a n)"),
                                             in_=lamp.rearrange("a n -> (a n)").partition_broadcast(128)),
                 w=[("lamb",)], dma=True)
            R.op("dve", lambda e: e.tensor_tensor(out=lamt[:, 0, :], in0=lamb[:, 0, :], in1=lamb[:, 1, :], op=ALU.mult),
                 r=[("lamb",)], w=[("lamt", 0)])
            R.op("dve", lambda e: e.tensor_tensor(out=lamt[:, 1, :], in0=lamb[:, 2, :], in1=lamb[:, 3, :], op=ALU.mult),
                 r=[("lamb",)], w=[("lamt", 1)])
            R.op("dve", lambda e: e.reduce_sum(out=lams[:, 0:2], in_=lamt, axis=mybir.AxisListType.X),
                 r=[("lamt", 0), ("lamt", 1)], w=[("lams",)])
            R.op("act", lambda e: e.activation(out=lams[:, 2:4], in_=lams[:, 0:2], func=AF.Exp, bias=zero_c, scale=1.0),
                 r=[("lams",), ("cols",)], w=[("lams", 1)])
            R.op("dve", lambda e: e.scalar_tensor_tensor(out=nlam_c, in0=lams[:, 3:4], scalar=-LAMBDA_INIT1,
                                                         in1=lams[:, 2:3], op0=ALU.add, op1=ALU.subtract),
                 r=[("lams", 1), ("cols",)], w=[("cols", 3)])
            R.op("dve", lambda e: e.tensor_scalar(out=subg_c, in0=smb[:, 3 * KC:3 * KC + 1], scalar1=1.0 - LAMBDA_INIT1,
                                                  scalar2=None, op0=ALU.mult),
                 r=[("smb",), ("cols",)], w=[("cols", 4)])
            R.op("pool", lambda e: e.iota(Ti, [[1, TWID]], base=-TOFF, channel_multiplier=-1),
                 w=[("T",)])
            R.op("pool", lambda e: e.tensor_copy(out=Tt, in_=Ti), r=[("T",)], w=[("T",)])
            R.op("act", lambda e: e.activation(out=Tt, in_=Tt, func=AF.Abs, bias=smb[:, 3 * KC + 1:3 * KC + 2], scale=1.0),
                 r=[("T",), ("smb",)], w=[("T",)])

            def kv_src(h):
                if mode == "B":
                    return [Kf[0, h], Kf[1, h]], [Vf[0, h], Vf[1, h]]
                return [KVf[h, 0][0:128, :], KVf[h, 0][128:256, :]], [KVf[h, 1][0:128, :], KVf[h, 1][128:256, :]]

            def att_load(t, h, par):
                ks, vs = kv_src(h)
                dep = [("KVf", h, 0)] if mode == "ALL" else []
                depv = [("KVf", h, 1)] if mode == "ALL" else []
                for r_ in range(2):
                    R.op("sp", lambda e, r_=r_, ks=ks: e.dma_start(out=kTb[par][:, r_ * HALF:(r_ + 1) * HALF], in_=ks[r_]),
                         r=dep, w=[("kT", par, r_)], dma=True)
                    R.op("sp", lambda e, r_=r_, vs=vs: e.dma_start(
                        out=vhb[par][:, r_ * 16:(r_ + 1) * 16, :], in_=vs[r_].rearrange("p (k n) -> p k n", k=16)),
                        r=depv, w=[("vh", par, r_)], dma=True)
                for c in range(2):
                    R.op("sp", lambda e, c=c: e.dma_start(out=qTb[par][c][c * 64:(c + 1) * 64, :],
                                                          in_=Qs[h][c * 64:(c + 1) * 64, t * TW:(t + 1) * TW]),
                         r=[("QK", 0, h, t)], w=[("qT", par, c)], dma=True)

            for par_ in range(2):
                for c in range(2):
                    R.op("pool", lambda e, par_=par_, c=c: e.memset(qTb[par_][c], 0.0), w=[("qT", par_, c)])
            LA = 3
            NSC, NE = 3, 5
            SBANKS = (0, 1, 2, 7)
            gstep = [0]
            for t in range(NTL if DEBUG["Bstage"] >= 1 else 0):
                NHL = DEBUG["NHL"]
                att_load(t, 0, 0)
                steps = [(h, sub, kt) for h in range(NHL) for sub in range(2) for kt in range(32)]
                info = {}

                def s_stage(i, t=t):
                    h, sub, kt = steps[i]
                    par = h % 2
                    if sub == 0 and kt == LA and h + 1 < NHL:
                        att_load(t, h + 1, 1 - par)
                    slope8 = -8.0 * float(2.0 ** (-8.0 * (h + 1) / NH))
                    kT, qT = kTb[par], qTb[par]
                    qs = sub * 256
                    g = gstep[0]
                    gstep[0] += 1
                    sbi = SBANKS[g % 4]
                    Sb, Sk = banks[sbi], ("ps", sbi)
                    r_ = kt // 16
                    R.op("pe", lambda e, kt=kt, Sb=Sb, kT=kT, q2=qT2[par], qs=qs: e.matmul(
                        Sb.rearrange("p (c n) -> p c n", c=2), lhsT=kT[:, kt * 128:(kt + 1) * 128],
                        rhs=q2[:, :, qs:qs + 256], start=True, stop=True),
                        r=[("kT", par, r_), ("qT", par, 0), ("qT", par, 1)], w=[Sk])
                    x0 = t * TW + qs - kt * 128 + TOFF
                    sc, sck = scb[g % NSC], ("sc", g % NSC)
                    R.op("dve", lambda e, Sb=Sb, sc=sc, x0=x0, slope8=slope8: e.scalar_tensor_tensor(
                        out=sc.rearrange("p (c n) -> p c n", c=2),
                        in0=Tt[:, x0:x0 + 256].unsqueeze(1).to_broadcast([128, 2, 256]),
                        scalar=slope8, in1=Sb.rearrange("p (c n) -> p c n", c=2),
                        op0=ALU.mult, op1=ALU.add),
                        r=[Sk, ("T",)], w=[sck])
                    Et, Ek = Eb[g % NE], ("E", g % NE)
                    R.op("act", lambda e, sc=sc, Et=Et: e.activation(out=Et, in_=sc, func=AF.Exp,
                                                                     bias=shift_c, scale=0.125),
                         r=[sck, ("cols",)], w=[Ek])
                    info[i] = (Et, Ek)

                def av_stage(i, t=t):
                    h, sub, kt = steps[i]
                    par = h % 2
                    vh = vhb[par]
                    qs = sub * 256
                    oh = ohb[h % 2]
                    Et, Ek = info.pop(i)
                    Ob, Ok = (banks[3], ("ps", 3)) if sub == 0 else (banks[5], ("ps", 5))
                    Zb, Zk = (banks[4], ("ps", 4)) if sub == 0 else (banks[6], ("ps", 6))
                    R.op("pe", lambda e, kt=kt, Et=Et, Ob=Ob, vh=vh: e.matmul(
                        Ob, lhsT=vh[:, kt, :], rhs=Et, start=(kt == 0), stop=(kt == 31)),
                        r=[("vh", par, kt // 16), Ek], w=[Ok])
                    R.op("pe", lambda e, kt=kt, Et=Et, Zb=Zb: e.matmul(
                        Zb, lhsT=ones, rhs=Et, start=(kt == 0), stop=(kt == 31)),
                        r=[("ones",), Ek], w=[Zk])
                    if kt < 31:
                        return
                    def stA(Zb=Zb, Zk=Zk):
                        R.op("act", lambda e: e.activation(out=rzb, in_=Zb, func=AF.Ln, bias=zero_c, scale=1.0),
                             r=[Zk, ("cols",)], w=[("rz",)])

                    def stA2():
                        R.op("act", lambda e: e.activation(out=rzb, in_=rzb, func=AF.Exp, bias=zero_c, scale=-1.0),
                             r=[("rz",), ("cols",)], w=[("rz",)])

                    def stB(Ob=Ob, Ok=Ok):
                        R.op("dve", lambda e: e.tensor_tensor(out=tb, in0=Ob, in1=rzb, op=ALU.mult),
                             r=[Ok, ("rz",)], w=[("tb",)])

                    def stB2(oh=oh, qs=qs, h=h, sub=sub):
                        R.op("dve", lambda e: e.scalar_tensor_tensor(
                            out=oh[:, qs:qs + 256], in0=tb[:, 256:512], scalar=nlam_c, in1=tb[:, 0:256],
                            op0=ALU.mult, op1=ALU.add),
                            r=[("tb",), ("cols", 3)], w=[("oh", h % 2, sub)])
                    stages = [stA, stA2, stB, stB2]
                    if sub == 1:
                        ohk = [("oh", h % 2, 0), ("oh", h % 2, 1)]

                        def stC(oh=oh, ohk=ohk):
                            R.op("act", lambda e: e.activation(out=osq, in_=oh, func=AF.Square, bias=zero_c, scale=1.0),
                                 r=ohk + [("cols",)], w=[("osq",)])

                        def stD():
                            R.op("pe", lambda e: e.matmul(banks[7], lhsT=ones, rhs=osq, start=True, stop=True),
                                 r=[("ones",), ("osq",)], w=[("ps", 7)])

                        def stE():
                            R.op("act", lambda e: e.activation(out=rstd[:, 1:TW + 1], in_=banks[7], func=AF.Ln, bias=seps_c,
                                                               scale=1.0 / 128),
                                 r=[("ps", 7), ("cols", 2)], w=[("rstd",)])

                        def stE2():
                            R.op("act", lambda e: e.activation(out=rstd[:, 1:TW + 1], in_=rstd[:, 1:TW + 1], func=AF.Exp,
                                                               bias=zero_c, scale=-0.5),
                                 r=[("rstd",), ("cols",)], w=[("rstd",)])

                        def stF(oh=oh, h=h, ohk=ohk):
                            R.op("dve", lambda e: e.scalar_tensor_tensor(
                                out=hn[:, h, 1:TW + 1], in0=oh, scalar=subg_c, in1=rstd[:, 1:TW + 1], op0=ALU.mult, op1=ALU.mult),
                                r=ohk + [("cols", 4), ("rstd",)], w=[("hn", h)])
                        stages += [stC, (lambda: (stD(), stE())), stE2, stF]
                    for n_, f_ in enumerate(stages):
                        pending.append((i + 2 + 2 * n_, f_))

                pending = []
                for i in range(len(steps) + LA):
                    if i < len(steps):
                        s_stage(i)
                    if i - LA >= 0:
                        av_stage(i - LA)
                        while pending and pending[0][0] <= i - LA:
                            pending.pop(0)[1]()
                while pending:
                    pending.pop(0)[1]()
                if DEBUG["Bstage"] < 2:
                    R.op("sp", lambda e, t=t: e.dma_start(out=outT[t].rearrange("p (k n) -> p k n", k=KC)[:, :, 0:257], in_=hn[:, :, 0:514].bitcast(F32)),
                         r=[("hn", kc) for kc in range(KC)], w=[("out", t)], dma=True)
                    continue
                R.op("sp", lambda e, t=t: e.dma_start(out=xh[:, :, 1:TW + 1], in_=h1s[t].rearrange("p (k n) -> p k n", k=KC)),
                     r=[("h1s", t)], w=[("xh", kc) for kc in range(KC)], dma=True)
                resid_linear(P_WO, 1, lambda gk: hn[:, gk, 1:TW + 1], lambda gk: ("hn", gk))
                rmsnorm(3, False)
                mlp(P_MLP1)
                rmsnorm(4, False)
                for kc in range(KC):
                    R.op("dve", lambda e, kc=kc: e.scalar_tensor_tensor(
                        out=xh[:, kc, 1:TW + 1], in0=xh[:, kc, 1:TW + 1], scalar=gsb[:, 4 * KC + kc:4 * KC + kc + 1],
                        in1=rstd[:, 1:TW + 1], op0=ALU.mult, op1=ALU.mult),
                        r=[("xh", kc), ("gsb",), ("rstd",), ("hn", kc)], w=[("xh", kc)])
                R.op("sp", lambda e, t=t: e.dma_start(out=outT[t].rearrange("p (k n) -> p k n", k=KC), in_=xh[:, :, 1:TW + 1]),
                     r=[("xh", kc) for kc in range(KC)], w=[("out", t)], dma=True)

        pp_advance(NCH)
        assert wst["use"] == len(seq), (wst, len(seq))
        fin = Item("sp", None, set(R.dma_last.values()), False)
        for d in fin.deps:
            d.needed = True
        R.q["sp"].append(fin)
        with nc.Block() as block:
            R.emit(nc, block, semh)
        print("SEMSTAT", mode, {e: (len(R.q[e]), max([it.val for it in R.q[e] if not it.is_dma] + [0])) for e in Rec.ENGS},
              "dma", max(R.dma_cnt.values()), "sbuf", off[0] * 2)
    return nc


def _panels(W, r0s, c0s):
    out = np.empty((len(r0s), 128, 2048), np.float32)
    for i, (r0, c0) in enumerate(zip(r0s, c0s)):
        out[i] = W[r0:r0 + 2048, c0:c0 + 128].reshape(16, 128, 128).transpose(1, 0, 2).reshape(128, 2048)
    return out


def _mlp_panels(w1, w2):
    ps = []
    for half in range(2):
        cs = [(half * 32 + fo) * 128 for fo in range(32)]
        ps.append(_panels(w1, [0] * 32, cs))
        r0s, c0s = [], []
        for do in range(16):
            for s in range(2):
                r0s.append(half * 4096 + s * 2048)
                c0s.append(do * 128)
        ps.append(_panels(w2, r0s, c0s))
    return np.concatenate(ps, 0)


def _layout_inputs(inp):
    x = np.asarray(inp["x"], np.float32)
    w_in = np.asarray(inp["conv_w_in"][0], np.float32)
    cs = []
    for j in range(16):
        cs += [(16 + j) * 128, (32 + j) * 128, j * 128]
    pan = [_panels(w_in, [0] * 48, cs)]
    pan.append(_panels(np.asarray(inp["conv_w_out"][0], np.float32), [0] * 16, [d * 128 for d in range(16)]))
    pan.append(_mlp_panels(np.asarray(inp["mlp_w1"][0], np.float32), np.asarray(inp["mlp_w2"][0], np.float32)))
    cs = []
    for h in range(16):
        cs += [h * 128, (16 + h) * 128, (32 + h) * 128]
    pan.append(_panels(np.asarray(inp["attn_w_qkv"][0], np.float32), [0] * 48, cs))
    pan.append(_panels(np.asarray(inp["attn_w_o"][0], np.float32), [0] * 16, [d * 128 for d in range(16)]))
    pan.append(_mlp_panels(np.asarray(inp["mlp_w1"][1], np.float32), np.asarray(inp["mlp_w2"][1], np.float32)))
    wpan = np.concatenate(pan, 0)
    assert wpan.shape[0] == NPAN
    fm = lambda v: np.asarray(v, np.float32).reshape(16, 128).T
    gains = np.concatenate([fm(inp["ln_mix"][0]), fm(inp["ln_mlp"][0]), fm(inp["ln_mix"][1]),
                            fm(inp["ln_mlp"][1]), fm(inp["ln_f"])], 1)
    cw = np.asarray(inp["conv_w"][0], np.float32)
    convw = np.concatenate([fm(cw[0]), fm(cw[1]), fm(cw[2])], 1)
    subln = np.asarray(inp["attn_subln"][0], np.float32).reshape(128, 1)
    lamp = np.stack([np.asarray(inp[k][0], np.float32) for k in
                     ("attn_lambda_q1", "attn_lambda_k1", "attn_lambda_q2", "attn_lambda_k2")], 0)
    per_core = []
    xp = np.zeros((4, SEQ + 2, D), np.float32)
    xp[:, 1:SEQ + 1] = x
    for c in range(NCORE):
        b, hf = c // 2, c % 2
        xt = np.empty((NT, 128, KC, TWH), np.float32)
        for t in range(NT):
            s0 = hf * HALF + t * TW
            blk = xp[b, s0:s0 + TWH]
            xt[t] = blk.T.reshape(KC, 128, TWH).transpose(1, 0, 2)
        smalls = np.concatenate([convw, subln, np.full((128, 1), hf * HALF, np.float32)], 1)
        per_core.append({"xT": xt.reshape(NT, 128, KC * TWH), "smalls": np.ascontiguousarray(smalls)})
    return wpan, np.ascontiguousarray(gains), np.ascontiguousarray(lamp), per_core


def _assemble(outs):
    out = np.empty((4, SEQ, D), np.float32)
    for c in range(NCORE):
        b, hf = c // 2, c % 2
        o = outs[c].reshape(NT, 128, KC, TW)
        for t in range(NT):
            s0 = hf * HALF + t * TW
            out[b, s0:s0 + TW] = o[t].transpose(2, 1, 0).reshape(TW, D)
    return out


FUSED = True
DEBUG = {"NT": NT, "stop": 99, "Bstage": 2, "NHL": NH, "v": 0}
_cache = {}


def _get(mode):
    if mode not in _cache:
        _cache[mode] = build(mode)
    return _cache[mode]


def kernel(**inp):
    wpan, gains, lamp, pc = _layout_inputs(inp)
    cores = list(range(NCORE))
    if FUSED:
        nc = _get("ALL")
        maps = [{"wpan": wpan, "gains": gains, "lamp": lamp, "smalls": pc[c]["smalls"], "xT": pc[c]["xT"]}
                for c in cores]
        res = run_bass_kernel_spmd(nc, maps, core_ids=cores)
        return _assemble([np.asarray(r["outT"]) for r in res.results])
    ncA = _get("A")
    wA = np.ascontiguousarray(wpan[:P_WO])
    maps = [{"wpan": wA, "gains": gains, "lamp": lamp, "smalls": pc[c]["smalls"], "xT": pc[c]["xT"]} for c in cores]
    ra = run_bass_kernel_spmd(ncA, maps, core_ids=cores).results
    ncB = _get("B")
    wB = np.ascontiguousarray(wpan[P_WO:])
    maps = []
    for c in cores:
        c0 = (c // 2) * 2
        Kf = np.stack([np.asarray(ra[c0]["Ks"]), np.asarray(ra[c0 + 1]["Ks"])], 0)
        Vf = np.stack([np.asarray(ra[c0]["Vs"]), np.asarray(ra[c0 + 1]["Vs"])], 0)
        maps.append({"wpan": wB, "gains": gains, "lamp": lamp, "smalls": pc[c]["smalls"],
                     "h1s": np.asarray(ra[c]["h1s"]), "Qs": np.asarray(ra[c]["Qs"]), "Kf": Kf, "Vf": Vf})
    rb = run_bass_kernel_spmd(ncB, maps, core_ids=cores).results
    return _assemble([np.asarray(r["outT"]) for r in rb])
```

```python
import numpy as np
from contextlib import ExitStack
import concourse.bass as bass
import concourse.mybir as mybir
from concourse.bass_utils import run_bass_kernel_spmd

F32 = mybir.dt.float32
BF16 = mybir.dt.bfloat16
I32 = mybir.dt.int32
ALU = mybir.AluOpType
AF = mybir.ActivationFunctionType

D = 2048
KC = 16
NT = 4
TW = 512
TWH = 514
NCORE = 8
SEQ = 4096
HALF = 2048
NH = 16
EPS = 1e-6
SUBEPS = 1e-5
LAMBDA_INIT1 = 0.8 - 0.6 * float(np.exp(-0.3 * 1))
TOFF = 3968
TWID = 6016
NDMASEM = 8

P_WIN, P_WOUT, P_MLP0, P_QKV, P_WO, P_MLP1 = 0, 48, 64, 192, 240, 256
NPAN = 384


class Item:
    __slots__ = ("eng", "fn", "deps", "needed", "val", "sem", "is_dma", "inc")

    def __init__(self, eng, fn, deps, is_dma):
        self.eng, self.fn, self.deps, self.is_dma = eng, fn, deps, is_dma
        self.needed = False
        self.val = 0
        self.sem = None
        self.inc = 16


class Rec:
    ENGS = ("sp", "act", "pe", "dve", "pool")

    def __init__(self):
        self.q = {e: [] for e in self.ENGS}
        self.lastw = {}
        self.reads = {}
        self.dma_rot = {e: 0 for e in self.ENGS}
        self.dma_last = {}
        self.dma_cnt = {}

    def op(self, eng, fn, r=(), w=(), dma=False, cc=False, after=()):
        deps = set()
        for k in after:
            t = self.lastw.get(k)
            if t is not None:
                deps.add(t)
        for k in r:
            t = self.lastw.get(k)
            if t is not None:
                deps.add(t)
        for k in w:
            t = self.lastw.get(k)
            if t is not None:
                deps.add(t)
            for t in self.reads.get(k, ()):
                deps.add(t)
        if eng == "pe" and not dma:
            deps = {d for d in deps if d.is_dma or d.eng != "pe"}
        it = Item(eng, fn, deps, dma)
        if cc:
            it.sem = ("cc", 0)
            self.dma_cnt[it.sem] = self.dma_cnt.get(it.sem, 0) + 1
            it.val = self.dma_cnt[it.sem]
            it.inc = 1
            self.dma_last[it.sem] = it
        elif dma:
            j = self.dma_rot[eng] % NDMASEM
            self.dma_rot[eng] += 1
            sem = (eng, j)
            prev = self.dma_last.get(sem)
            if prev is not None:
                deps.add(prev)
            self.dma_last[sem] = it
            self.dma_cnt[sem] = self.dma_cnt.get(sem, 0) + 16
            it.sem = sem
            it.val = self.dma_cnt[sem]
        deps.discard(it)
        for d in deps:
            d.needed = True
        self.q[eng].append(it)
        for k in r:
            lst = self.reads.setdefault(k, [])
            if not dma:
                for i_, o in enumerate(lst):
                    if o.eng == eng and not o.is_dma:
                        lst[i_] = it
                        break
                else:
                    lst.append(it)
            else:
                lst.append(it)
        for k in w:
            self.lastw[k] = it
            self.reads[k] = []
        return it

    def emit(self, nc, block, semh):
        for eng in self.ENGS:
            cnt = 0
            for it in self.q[eng]:
                if not it.is_dma and it.needed:
                    cnt += 1
                    it.sem = eng
                    it.val = cnt
        bname = {"sp": "sync", "act": "scalar", "pe": "tensor", "dve": "vector", "pool": "gpsimd"}
        for eng in self.ENGS:
            items = self.q[eng]

            def body(e, items=items, eng=eng):
                known = {}
                for it in items:
                    waits = {}
                    for d in it.deps:
                        if eng == "pe" and d.eng == "pe" and not d.is_dma:
                            continue
                        if waits.get(d.sem, 0) < d.val:
                            waits[d.sem] = d.val
                    for s, v in waits.items():
                        if known.get(s, 0) >= v:
                            continue
                        known[s] = v
                        e.wait_ge(semh[s], v)
                    if it.fn is None:
                        continue
                    ins = it.fn(e)
                    if it.is_dma:
                        ins.then_inc(semh[it.sem], it.inc)
                    elif it.needed:
                        ins.then_inc(semh[eng], 1)

            getattr(block, bname[eng])(body)


def build(mode):
    doA = mode in ("A", "ALL")
    doB = mode in ("B", "ALL")
    NTL = DEBUG["NT"]
    STOP = DEBUG["stop"]
    nc = bass.Bass("TRN2", target_bir_lowering=False)
    R = Rec()

    def dt_in(name, shape, dt=F32):
        return nc.dram_tensor(name, list(shape), dt, kind="ExternalInput").ap()

    def dt_out(name, shape, dt=F32):
        return nc.dram_tensor(name, list(shape), dt, kind="ExternalOutput").ap()

    def dt_int(name, shape, dt=F32):
        return nc.dram_tensor(name, list(shape), dt).ap()

    p_lo = 0 if doA else P_WO
    p_hi = NPAN if doB else P_WO
    npan = p_hi - p_lo
    wpan = dt_in("wpan", [npan, 128, 2048])
    wbf = dt_int("wbf", [npan, 128, 2048], BF16)
    gains = dt_in("gains", [128, 5 * KC])
    smalls = dt_in("smalls", [128, 3 * KC + 1 + 1])
    if doB:
        lamp = dt_in("lamp", [4, 64])
    if doA:
        xT = dt_in("xT", [NT, 128, KC * TWH])
    if mode == "A":
        h1s = dt_out("h1s", [NT, 128, KC * TW])
        Qs = dt_out("Qs", [NH, 128, HALF], BF16)
        Ks = dt_out("Ks", [NH, 128, HALF], BF16)
        Vs = dt_out("Vs", [NH, 128, KC * 128], BF16)
    elif mode == "B":
        h1s = dt_in("h1s", [NT, 128, KC * TW])
        Qs = dt_in("Qs", [NH, 128, HALF], BF16)
        Kf = dt_in("Kf", [2, NH, 128, HALF], BF16)
        Vf = dt_in("Vf", [2, NH, 128, KC * 128], BF16)
    else:
        h1s = dt_int("h1s", [NT, 128, KC * TW])
        Qs = dt_int("Qs", [NH, 128, HALF], BF16)
        KVs = dt_int("KVs", [NH, 2, 128, HALF], BF16)
        KVf = dt_int("KVf", [NH, 2, 2 * 128, HALF], BF16)
        Ks = [KVs[h, 0] for h in range(NH)]
        Vs = [KVs[h, 1] for h in range(NH)]
    if doB:
        outT = dt_out("outT", [NT, 128, KC * TW])

    es = ExitStack()
    with es:
        ARENA = 212480 // 2
        arena = es.enter_context(nc.sbuf_tensor("arena", [128, ARENA], BF16))
        off = [0]

        def carve(nelem, dt):
            nb = nelem * (4 if dt in (F32, I32) else 2)
            nb = (nb + 63) // 64 * 64
            o = off[0]
            off[0] += nb // 2
            assert off[0] <= ARENA, ("sbuf overflow", off[0] * 2)
            v = arena[:, o:o + nb // 2]
            if dt != BF16:
                v = v.bitcast(dt)
            return v[:, 0:nelem]

        xh = carve(KC * TWH, F32).rearrange("p (k n) -> p k n", k=KC)
        hn = carve(KC * TWH, BF16).rearrange("p (k n) -> p k n", k=KC)
        abuf = carve(32 * TWH, BF16).rearrange("p (k n) -> p k n", k=32)
        wring = [carve(2048, BF16).rearrange("p (k n) -> p k n", k=KC) for _ in range(4)]
        st_o = off[0]
        Tt = carve(TWID, F32)
        Ti = arena[:, st_o:st_o + 2 * TWID].bitcast(I32)
        st32 = [arena[:, st_o + i * 2048:st_o + (i + 1) * 2048].bitcast(F32) for i in range(2)]
        st16 = [arena[:, st_o + 4096 + i * 1024:st_o + 4096 + (i + 1) * 1024] for i in range(2)]
        rstd = carve(TWH, F32)
        csb = [carve(TWH, F32) for _ in range(2)]
        upb = [carve(TWH, F32) for _ in range(2)]
        accb = [carve(TW, F32) for _ in range(2)]
        rbuf = [carve(TW, F32) for _ in range(2)]
        qkvb = [carve(TW, BF16) for _ in range(3)]
        vtok = [carve(TW, BF16) for _ in range(2)]
        kTb = [carve(SEQ, BF16) for _ in range(2)]
        vhb = [carve(SEQ, BF16).rearrange("p (k n) -> p k n", k=32) for _ in range(2)]
        qTo = [off[0], off[0] + 2 * TW]
        qTb = [[carve(TW, BF16) for _ in range(2)] for _ in range(2)]
        qT2 = [arena[:, qTo[p_]:qTo[p_] + 2 * TW].rearrange("p (c n) -> p c n", c=2) for p_ in range(2)]
        scb = [carve(TW, F32) for _ in range(3)]
        Eb = [carve(TW, BF16) for _ in range(5)]
        rzb = carve(TW, F32)
        tb = carve(TW, F32)
        ohb = [carve(TW, F32) for _ in range(2)]
        osq = carve(TW, BF16)
        ones = carve(128, BF16)
        ident = carve(128, BF16)
        gsb = carve(5 * KC, F32)
        smb = carve(3 * KC + 2, F32)
        lamb = carve(4 * 64, F32).rearrange("p (a n) -> p a n", a=4)
        lamt = carve(2 * 64, F32).rearrange("p (a n) -> p a n", a=2)
        lams = carve(4, F32)
        cols = carve(8, F32)
        iot = tb.bitcast(I32)[:, 0:128]
        iotf = rzb[:, 0:128]
        ST_KEYS = [("st32", 0), ("st32", 1), ("st16", 0), ("st16", 1)]

        banks = [es.enter_context(nc.psum_tensor(f"bank{i}", [128, TW], F32))[:] for i in range(8)]
        semh = {}
        for e in Rec.ENGS:
            semh[e] = es.enter_context(nc.semaphore(f"s_{e}"))
        for e in ("sp", "pool", "act"):
            for j in range(NDMASEM):
                semh[(e, j)] = es.enter_context(nc.semaphore(f"d_{e}{j}"))

        semh[("cc", 0)] = es.enter_context(nc.semaphore("s_cc"))
        zero_c, eps_c, seps_c, nlam_c, subg_c, shift_c = (cols[:, i:i + 1] for i in range(6))

        R.op("pool", lambda e: e.memset(ones, 1.0), w=[("ones",)])
        R.op("pool", lambda e: e.memset(cols, 0.0), w=[("cols",)])
        R.op("pool", lambda e: e.memset(eps_c, EPS), r=[("cols",)], w=[("cols", 1)])
        R.op("pool", lambda e: e.memset(seps_c, SUBEPS), r=[("cols",)], w=[("cols", 2)])
        R.op("pool", lambda e: e.iota(iot, [[1, 128]], base=0, channel_multiplier=-1), w=[("tb",)])
        R.op("pool", lambda e: e.tensor_copy(out=iotf, in_=iot), r=[("tb",)], w=[("rz",)])
        R.op("pool", lambda e: e.tensor_scalar(out=ident, in0=iotf, scalar1=0.0, scalar2=None, op0=ALU.is_equal),
             r=[("rz",)], w=[("ident",)])
        R.op("sp", lambda e: e.dma_start(out=gsb, in_=gains), w=[("gsb",)], dma=True)
        R.op("sp", lambda e: e.dma_start(out=smb, in_=smalls), w=[("smb",)], dma=True)
        CONST_KEYS = [("cols",), ("cols", 1), ("cols", 2), ("gsb",), ("smb",), ("ones",), ("ident",)]

        NCH = npan

        def pp_advance(upto):
            pass

        n_direct = (P_WO - p_lo) if doA else 0
        bg_state = {"done": False}

        bg_state["next"] = n_direct
        bg_state["armed"] = False
        BG_EVERY = 4

        def bg_one(pace_key=None):
            pid_ = bg_state["next"]
            if pid_ >= npan:
                return
            bg_state["next"] += 1
            R.op("pool", lambda e, pid_=pid_: e.dma_start(out=wbf[pid_], in_=wpan[pid_]),
                 after=([pace_key] if pace_key is not None else []), w=[("wbf", pid_)], dma=True)

        def emit_background():
            bg_state["armed"] = True

        def bg_flush():
            while bg_state["next"] < npan:
                bg_one()

        if n_direct == 0:
            bg_flush()
        direct_done = set()

        seq = []
        if doA:
            for t in range(NTL):
                if STOP >= 2:
                    seq += list(range(P_WIN, P_WIN + 48))
                if STOP >= 3:
                    seq += list(range(P_WOUT, P_WOUT + 16))
                if STOP >= 4:
                    seq += list(range(P_MLP0, P_MLP0 + 128))
                if STOP >= 5:
                    seq += list(range(P_QKV, P_QKV + 48))
        if doB and DEBUG["Bstage"] >= 2:
            for t in range(NTL):
                seq += list(range(P_WO, P_WO + 16)) + list(range(P_MLP1, P_MLP1 + 128))
        seq = [p - p_lo for p in seq]
        wst = {"ld": 0, "use": 0}

        def w_prefetch(upto):
            upto = min(upto, len(seq))
            while wst["ld"] < upto:
                i = wst["ld"]
                pid = seq[i]
                if pid < n_direct and pid not in direct_done:
                    direct_done.add(pid)
                    R.op("pool", lambda e, i=i, pid=pid: e.dma_start(
                        out=wring[i % 4], in_=wpan[pid].rearrange("p (k n) -> p k n", k=KC)),
                        w=[("w", i % 4)], dma=True)
                    R.op("sp", lambda e, i=i, pid=pid: e.dma_start(
                        out=wbf[pid].rearrange("p (k n) -> p k n", k=KC), in_=wring[i % 4]),
                        r=[("w", i % 4)], w=[("wbf", pid)], dma=True)
                    if len(direct_done) == n_direct:
                        emit_background()
                else:
                    R.op("sp", lambda e, i=i, pid=pid: e.dma_start(
                        out=wring[i % 4], in_=wbf[pid].rearrange("p (k n) -> p k n", k=KC)),
                        r=[("wbf", pid)], w=[("w", i % 4)], dma=True)
                wst["ld"] += 1

        def w_next(expect):
            i = wst["use"]
            assert seq[i] == expect - p_lo, (i, seq[i], expect)
            w_prefetch(i + 4)
            wst["use"] += 1
            if bg_state["armed"] and i % BG_EVERY == 0:
                bg_one(("w", i % 4))
            return wring[i % 4], ("w", i % 4)

        psrot = [0]

        def ps_next():
            b = psrot[0] % 5
            psrot[0] += 1
            return banks[b], ("ps", b)

        def mm_group(pan_ids, rhs_fn, rhs_keys, ps, pskey, extra=None):
            n = len(pan_ids) * KC
            idx = 0
            for pi, pid in enumerate(pan_ids):
                wt, wkey = w_next(pid)
                for kc in range(KC):
                    gk = pi * KC + kc
                    R.op("pe", lambda e, wt=wt, kc=kc, gk=gk, idx=idx: e.matmul(
                        ps, lhsT=wt[:, kc, :], rhs=rhs_fn(gk), start=(idx == 0), stop=(idx == n - 1)),
                        r=[wkey, rhs_keys(gk)], w=[pskey])
                    if extra is not None:
                        extra(wt, wkey, kc, gk, idx, n)
                    idx += 1

        def rmsnorm(gi, halo, src_keys=None, write_hn=True):
            c0, c1 = (0, TWH) if halo else (1, TW + 1)
            for kc in range(KC):
                R.op("act", lambda e, kc=kc: e.activation(out=abuf[:, 16 + kc, c0:c1], in_=xh[:, kc, c0:c1],
                                                          func=AF.Square, bias=zero_c, scale=1.0),
                     r=[("xh", kc), ("cols",)], w=[("a", 16 + kc)])
            for kc in range(KC):
                R.op("pe", lambda e, kc=kc: e.matmul(banks[6], lhsT=ones, rhs=abuf[:, 16 + kc, 1:TW + 1],
                                                     start=(kc == 0), stop=(kc == KC - 1)),
                     r=[("ones",), ("a", 16 + kc)], w=[("ps", 6)])
                if halo:
                    R.op("pe", lambda e, kc=kc: e.matmul(banks[7][:, 0:2], lhsT=ones, rhs=abuf[:, 16 + kc, 0:TWH:TW + 1],
                                                         start=(kc == 0), stop=(kc == KC - 1)),
                         r=[("ones",), ("a", 16 + kc)], w=[("ps", 7)])
            R.op("act", lambda e: e.activation(out=rstd[:, 1:TW + 1], in_=banks[6], func=AF.Ln, bias=eps_c, scale=1.0 / D),
                 r=[("ps", 6), ("cols", 1)], w=[("rstd",)])
            if halo:
                R.op("act", lambda e: e.activation(out=rstd[:, 0:TWH:TW + 1], in_=banks[7][:, 0:2], func=AF.Ln,
                                                   bias=eps_c, scale=1.0 / D),
                     r=[("ps", 7), ("cols", 1)], w=[("rstd",)])
            R.op("act", lambda e: e.activation(out=rstd[:, c0:c1], in_=rstd[:, c0:c1], func=AF.Exp, bias=zero_c, scale=-0.5),
                 r=[("rstd",), ("cols",)], w=[("rstd",)])
            for kc in range(KC if write_hn else 0):
                R.op("dve", lambda e, kc=kc: e.scalar_tensor_tensor(
                    out=hn[:, kc, c0:c1], in0=xh[:, kc, c0:c1], scalar=gsb[:, gi * KC + kc:gi * KC + kc + 1],
                    in1=rstd[:, c0:c1], op0=ALU.mult, op1=ALU.mult),
                    r=[("xh", kc), ("gsb",), ("rstd",)], w=[("hn", kc)])

        def resid_linear(pbase, npan_per, rhs_fn, rhs_keys):
            for do in range(KC):
                ps, pk = ps_next()
                mm_group([pbase + do * npan_per + s for s in range(npan_per)], rhs_fn, rhs_keys, ps, pk)
                R.op("dve", lambda e, do=do, ps=ps: e.tensor_tensor(out=xh[:, do, 1:TW + 1], in0=ps,
                                                                    in1=xh[:, do, 1:TW + 1], op=ALU.add),
                     r=[pk, ("xh", do)], w=[("xh", do)])

        def mlp(pbase):
            for half in range(2):
                hb = pbase + half * 64
                for fo in range(32):
                    ps, pk = ps_next()
                    mm_group([hb + fo], lambda gk: hn[:, gk, 1:TW + 1], lambda gk: ("hn", gk), ps, pk)
                    rb = rbuf[fo % 2]
                    R.op("act", lambda e, ps=ps, rb=rb: e.activation(out=rb, in_=ps, func=AF.Relu, bias=zero_c, scale=1.0),
                         r=[pk, ("cols",)], w=[("rbuf", fo % 2)])
                    R.op("dve", lambda e, fo=fo, rb=rb: e.tensor_tensor(out=abuf[:, fo, 0:TW], in0=rb, in1=rb, op=ALU.mult),
                         r=[("rbuf", fo % 2)], w=[("a", fo)])
                resid_linear(hb + 32, 2, lambda gk: abuf[:, gk, 0:TW], lambda gk: ("a", gk))

        if doA:
            cw = smb
            if STOP < 99:
                dbg_x = dt_out("dbg_x", [128, KC * TWH])
                dbg_h = dt_out("dbg_h", [128, KC * TWH], BF16)
                dbg_a = dt_out("dbg_a", [128, 32 * TWH], BF16)
            for t in range(NTL):
                R.op("sp", lambda e, t=t: e.dma_start(out=xh, in_=xT[t].rearrange("p (k n) -> p k n", k=KC)),
                     w=[("xh", kc) for kc in range(KC)], dma=True)
                if STOP >= 1:
                    rmsnorm(0, True)
                for j in range(KC if STOP >= 2 else 0):
                    ps_c, pk_c = ps_next()
                    ps_u, pk_u = ps_next()
                    ps_b, pk_b = ps_next()
                    cs, up, acc = csb[j % 2], upb[j % 2], accb[j % 2]

                    def halo_mm(hb):
                        def f(wt, wkey, kc, gk, idx, n):
                            R.op("pe", lambda e: e.matmul(banks[hb][:, 0:2], lhsT=wt[:, kc, :],
                                                          rhs=hn[:, kc, 0:TWH:TW + 1], start=(idx == 0), stop=(idx == n - 1)),
                                 r=[wkey, ("hn", kc)], w=[("ps", hb)])
                        return f
                    rf = lambda gk: hn[:, gk, 1:TW + 1]
                    rk = lambda gk: ("hn", gk)
                    mm_group([P_WIN + 3 * j], rf, rk, ps_c, pk_c, extra=halo_mm(7))
                    R.op("act", lambda e, cs=cs, ps_c=ps_c: e.copy(out=cs[:, 1:TW + 1], in_=ps_c),
                         r=[pk_c, ("cols",)], w=[("cs", j % 2)])
                    R.op("act", lambda e, cs=cs: e.copy(out=cs[:, 0:TWH:TW + 1], in_=banks[7][:, 0:2]),
                         r=[("ps", 7), ("cols",)], w=[("csh", j % 2)])
                    mm_group([P_WIN + 3 * j + 1], rf, rk, ps_u, pk_u, extra=halo_mm(5))
                    R.op("dve", lambda e, cs=cs, up=up, ps_u=ps_u: e.tensor_tensor(
                        out=up[:, 1:TW + 1], in0=ps_u, in1=cs[:, 1:TW + 1], op=ALU.mult),
                        r=[pk_u, ("cs", j % 2)], w=[("up", j % 2)])
                    R.op("dve", lambda e, cs=cs, up=up: e.tensor_tensor(
                        out=up[:, 0:TWH:TW + 1], in0=banks[5][:, 0:2], in1=cs[:, 0:TWH:TW + 1], op=ALU.mult),
                        r=[("ps", 5), ("csh", j % 2)], w=[("uph", j % 2)])
                    R.op("dve", lambda e, up=up, acc=acc, j=j: e.tensor_scalar(
                        out=acc, in0=up[:, 1:TW + 1], scalar1=cw[:, KC + j:KC + j + 1], scalar2=None, op0=ALU.mult),
                        r=[("up", j % 2), ("smb",)], w=[("acc", j % 2)])
                    R.op("dve", lambda e, up=up, acc=acc, j=j: e.scalar_tensor_tensor(
                        out=acc, in0=up[:, 0:TW], scalar=cw[:, j:j + 1], in1=acc, op0=ALU.mult, op1=ALU.add),
                        r=[("up", j % 2), ("uph", j % 2), ("smb",), ("acc", j % 2)], w=[("acc", j % 2)])
                    R.op("dve", lambda e, up=up, acc=acc, j=j: e.scalar_tensor_tensor(
                        out=acc, in0=up[:, 2:TW + 2], scalar=cw[:, 2 * KC + j:2 * KC + j + 1], in1=acc,
                        op0=ALU.mult, op1=ALU.add),
                        r=[("up", j % 2), ("uph", j % 2), ("smb",), ("acc", j % 2)], w=[("acc", j % 2)])
                    mm_group([P_WIN + 3 * j + 2], rf, rk, ps_b, pk_b)
                    R.op("dve", lambda e, acc=acc, ps_b=ps_b, j=j: e.tensor_tensor(
                        out=abuf[:, j, 0:TW], in0=ps_b, in1=acc, op=ALU.mult),
                        r=[pk_b, ("acc", j % 2)], w=[("a", j)])
                if STOP >= 3:
                    resid_linear(P_WOUT, 1, lambda gk: abuf[:, gk, 0:TW], lambda gk: ("a", gk))
                if STOP >= 4:
                    rmsnorm(1, False)
                    mlp(P_MLP0)
                if STOP < 5:
                    allk = [("xh", kc) for kc in range(KC)] + [("hn", kc) for kc in range(KC)] + [("a", kc) for kc in range(32)]
                    R.op("sp", lambda e: e.dma_start(out=dbg_x.rearrange("p (k n) -> p k n", k=KC), in_=xh), r=allk, w=[("dbgx",)], dma=True)
                    R.op("sp", lambda e: e.dma_start(out=dbg_h.rearrange("p (k n) -> p k n", k=KC), in_=hn), r=allk, w=[("dbgh",)], dma=True)
                    R.op("sp", lambda e: e.dma_start(out=dbg_a.rearrange("p (k n) -> p k n", k=32), in_=abuf), r=allk, w=[("dbga",)], dma=True)
                    continue
                R.op("sp", lambda e, t=t: e.dma_start(out=h1s[t].rearrange("p (k n) -> p k n", k=KC), in_=xh[:, :, 1:TW + 1]),
                     r=[("xh", kc) for kc in range(KC)], w=[("h1s", t)], dma=True)
                rmsnorm(2, False)
                rf = lambda gk: hn[:, gk, 1:TW + 1]
                rk = lambda gk: ("hn", gk)
                cnt = 0
                for h in range(NH):
                    for which in range(3):
                        ps, pk = ps_next()
                        if which < 2:
                            mm_group([P_QKV + 3 * h + which], rf, rk, ps, pk)
                            qb = qkvb[cnt % 3]
                            qk = ("qkvb", cnt % 3)
                            cnt += 1
                            R.op("act", lambda e, qb=qb, ps=ps: e.copy(out=qb, in_=ps),
                                 r=[pk, ("cols",)], w=[qk])
                            dst = (Qs, Ks)[which]
                            R.op("sp", lambda e, qb=qb, dst=dst, h=h, t=t: e.dma_start(
                                out=dst[h][:, t * TW:(t + 1) * TW], in_=qb),
                                r=[qk], w=[("QK", which, h, t)], dma=True)
                        else:
                            wt, wkey = w_next(P_QKV + 3 * h + 2)
                            for s_ in range(4):
                                for kc in range(KC):
                                    R.op("pe", lambda e, wt=wt, kc=kc, s_=s_, ps=ps: e.matmul(
                                        ps[:, s_ * 128:(s_ + 1) * 128], lhsT=hn[:, kc, 1 + s_ * 128:1 + (s_ + 1) * 128],
                                        rhs=wt[:, kc, :], start=(kc == 0), stop=(kc == KC - 1)),
                                        r=[wkey, ("hn", kc)], w=[pk])
                            vt = vtok[h % 2]
                            R.op("act", lambda e, vt=vt, ps=ps: e.copy(out=vt, in_=ps),
                                 r=[pk, ("cols",)], w=[("vtok", h % 2)])
                            R.op("sp", lambda e, vt=vt, h=h, t=t: e.dma_start(
                                out=Vs[h].rearrange("p (k n) -> p k n", k=KC)[:, t * 4:(t + 1) * 4, :],
                                in_=vt.rearrange("p (k n) -> p k n", k=4)),
                                r=[("vtok", h % 2)], w=[("V", h, t)], dma=True)

        if mode == "ALL":
            for h in range(NH):
                for kv in range(2):
                    dep = [("QK", 1, h, t) for t in range(NT)] if kv == 0 else [("V", h, t) for t in range(NT)]
                    R.op("pool", lambda e, h=h, kv=kv: e.collective_compute(
                        "AllGather", ALU.bypass, replica_groups=[[0, 1], [2, 3], [4, 5], [6, 7]],
                        ins=[KVs[h, kv]], outs=[KVf[h, kv]]), r=dep, w=[("KVf", h, kv)], dma=True, cc=True)

        if doB:
            bg_flush()
            pp_advance(NCH)
            R.op("sp", lambda e: e.dma_start(out=lamb.rearrange("p a n -> p (a n)"),
                                             in_=lamp.rearrange("a n -> (a n)").partition_broadcast(128)),
                 w=[("lamb",)], dma=True)
            R.op("dve", lambda e: e.tensor_tensor(out=lamt[:, 0, :], in0=lamb[:, 0, :], in1=lamb[:, 1, :], op=ALU.mult),
                 r=[("lamb",)], w=[("lamt", 0)])
            R.op("dve", lambda e: e.tensor_tensor(out=lamt[:, 1, :], in0=lamb[:, 2, :], in1=lamb[:, 3, :], op=ALU.mult),
                 r=[("lamb",)], w=[("lamt", 1)])
            R.op("dve", lambda e: e.reduce_sum(out=lams[:, 0:2], in_=lamt, axis=mybir.AxisListType.X),
                 r=[("lamt", 0), ("lamt", 1)], w=[("lams",)])
            R.op("act", lambda e: e.activation(out=lams[:, 2:4], in_=lams[:, 0:2], func=AF.Exp, bias=zero_c, scale=1.0),
                 r=[("lams",), ("cols",)], w=[("lams", 1)])
            R.op("dve", lambda e: e.scalar_tensor_tensor(out=nlam_c, in0=lams[:, 3:4], scalar=-LAMBDA_INIT1,
                                                         in1=lams[:, 2:3], op0=ALU.add, op1=ALU.subtract),
                 r=[("lams", 1), ("cols",)], w=[("cols", 3)])
            R.op("dve", lambda e: e.tensor_scalar(out=subg_c, in0=smb[:, 3 * KC:3 * KC + 1], scalar1=1.0 - LAMBDA_INIT1,
                                                  scalar2=None, op0=ALU.mult),
                 r=[("smb",), ("cols",)], w=[("cols", 4)])
            R.op("pool", lambda e: e.iota(Ti, [[1, TWID]], base=-TOFF, channel_multiplier=-1),
                 w=[("T",)])
            R.op("pool", lambda e: e.tensor_copy(out=Tt, in_=Ti), r=[("T",)], w=[("T",)])
            R.op("act", lambda e: e.activation(out=Tt, in_=Tt, func=AF.Abs, bias=smb[:, 3 * KC + 1:3 * KC + 2], scale=1.0),
                 r=[("T",), ("smb",)], w=[("T",)])

            def kv_src(h):
                if mode == "B":
                    return [Kf[0, h], Kf[1, h]], [Vf[0, h], Vf[1, h]]
                return [KVf[h, 0][0:128, :], KVf[h, 0][128:256, :]], [KVf[h, 1][0:128, :], KVf[h, 1][128:256, :]]

            def att_load(t, h, par):
                ks, vs = kv_src(h)
                dep = [("KVf", h, 0)] if mode == "ALL" else []
                depv = [("KVf", h, 1)] if mode == "ALL" else []
                for r_ in range(2):
                    R.op("sp", lambda e, r_=r_, ks=ks: e.dma_start(out=kTb[par][:, r_ * HALF:(r_ + 1) * HALF], in_=ks[r_]),
                         r=dep, w=[("kT", par, r_)], dma=True)
                    R.op("sp", lambda e, r_=r_, vs=vs: e.dma_start(
                        out=vhb[par][:, r_ * 16:(r_ + 1) * 16, :], in_=vs[r_].rearrange("p (k n) -> p k n", k=16)),
                        r=depv, w=[("vh", par, r_)], dma=True)
                for c in range(2):
                    R.op("sp", lambda e, c=c: e.dma_start(out=qTb[par][c][c * 64:(c + 1) * 64, :],
                                                          in_=Qs[h][c * 64:(c + 1) * 64, t * TW:(t + 1) * TW]),
                         r=[("QK", 0, h, t)], w=[("qT", par, c)], dma=True)

            for par_ in range(2):
                for c in range(2):
                    R.op("pool", lambda e, par_=par_, c=c: e.memset(qTb[par_][c], 0.0), w=[("qT", par_, c)])
            LA = 3
            NSC, NE = 3, 5
            SBANKS = (0, 1, 2, 7)
            gstep = [0]
            for t in range(NTL if DEBUG["Bstage"] >= 1 else 0):
                NHL = DEBUG["NHL"]
                att_load(t, 0, 0)
                steps = [(h, sub, kt) for h in range(NHL) for sub in range(2) for kt in range(32)]
                info = {}

                def s_stage(i, t=t):
                    h, sub, kt = steps[i]
                    par = h % 2
                    if sub == 0 and kt == LA and h + 1 < NHL:
                        att_load(t, h + 1, 1 - par)
                    slope8 = -8.0 * float(2.0 ** (-8.0 * (h + 1) / NH))
                    kT, qT = kTb[par], qTb[par]
                    qs = sub * 256
                    g = gstep[0]
                    gstep[0] += 1
                    sbi = SBANKS[g % 4]
                    Sb, Sk = banks[sbi], ("ps", sbi)
                    r_ = kt // 16
                    R.op("pe", lambda e, kt=kt, Sb=Sb, kT=kT, q2=qT2[par], qs=qs: e.matmul(
                        Sb.rearrange("p (c n) -> p c n", c=2), lhsT=kT[:, kt * 128:(kt + 1) * 128],
                        rhs=q2[:, :, qs:qs + 256], start=True, stop=True),
                        r=[("kT", par, r_), ("qT", par, 0), ("qT", par, 1)], w=[Sk])
                    x0 = t * TW + qs - kt * 128 + TOFF
                    sc, sck = scb[g % NSC], ("sc", g % NSC)
                    R.op("dve", lambda e, Sb=Sb, sc=sc, x0=x0, slope8=slope8: e.scalar_tensor_tensor(
                        out=sc.rearrange("p (c n) -> p c n", c=2),
                        in0=Tt[:, x0:x0 + 256].unsqueeze(1).to_broadcast([128, 2, 256]),
                        scalar=slope8, in1=Sb.rearrange("p (c n) -> p c n", c=2),
                        op0=ALU.mult, op1=ALU.add),
                        r=[Sk, ("T",)], w=[sck])
                    Et, Ek = Eb[g % NE], ("E", g % NE)
                    R.op("act", lambda e, sc=sc, Et=Et: e.activation(out=Et, in_=sc, func=AF.Exp,
                                                                     bias=shift_c, scale=0.125),
                         r=[sck, ("cols",)], w=[Ek])
                    info[i] = (Et, Ek)

                def av_stage(i, t=t):
                    h, sub, kt = steps[i]
                    par = h % 2
                    vh = vhb[par]
                    qs = sub * 256
                    oh = ohb[h % 2]
                    Et, Ek = info.pop(i)
                    Ob, Ok = (banks[3], ("ps", 3)) if sub == 0 else (banks[5], ("ps", 5))
                    Zb, Zk = (banks[4], ("ps", 4)) if sub == 0 else (banks[6], ("ps", 6))
                    R.op("pe", lambda e, kt=kt, Et=Et, Ob=Ob, vh=vh: e.matmul(
                        Ob, lhsT=vh[:, kt, :], rhs=Et, start=(kt == 0), stop=(kt == 31)),
                        r=[("vh", par, kt // 16), Ek], w=[Ok])
                    R.op("pe", lambda e, kt=kt, Et=Et, Zb=Zb: e.matmul(
                        Zb, lhsT=ones, rhs=Et, start=(kt == 0), stop=(kt == 31)),
                        r=[("ones",), Ek], w=[Zk])
                    if kt < 31:
                        return
                    def stA(Zb=Zb, Zk=Zk):
                        R.op("act", lambda e: e.activation(out=rzb, in_=Zb, func=AF.Ln, bias=zero_c, scale=1.0),
                             r=[Zk, ("cols",)], w=[("rz",)])

                    def stA2():
                        R.op("act", lambda e: e.activation(out=rzb, in_=rzb, func=AF.Exp, bias=zero_c, scale=-1.0),
                             r=[("rz",), ("cols",)], w=[("rz",)])

                    def stB(Ob=Ob, Ok=Ok):
                        R.op("dve", lambda e: e.tensor_tensor(out=tb, in0=Ob, in1=rzb, op=ALU.mult),
                             r=[Ok, ("rz",)], w=[("tb",)])

                    def stB2(oh=oh, qs=qs, h=h, sub=sub):
                        R.op("dve", lambda e: e.scalar_tensor_tensor(
                            out=oh[:, qs:qs + 256], in0=tb[:, 256:512], scalar=nlam_c, in1=tb[:, 0:256],
                            op0=ALU.mult, op1=ALU.add),
                            r=[("tb",), ("cols", 3)], w=[("oh", h % 2, sub)])
                    stages = [stA, stA2, stB, stB2]
                    if sub == 1:
                        ohk = [("oh", h % 2, 0), ("oh", h % 2, 1)]

                        def stC(oh=oh, ohk=ohk):
                            R.op("act", lambda e: e.activation(out=osq, in_=oh, func=AF.Square, bias=zero_c, scale=1.0),
                                 r=ohk + [("cols",)], w=[("osq",)])

                        def stD():
                            R.op("pe", lambda e: e.matmul(banks[7], lhsT=ones, rhs=osq, start=True, stop=True),
                                 r=[("ones",), ("osq",)], w=[("ps", 7)])

                        def stE():
                            R.op("act", lambda e: e.activation(out=rstd[:, 1:TW + 1], in_=banks[7], func=AF.Ln, bias=seps_c,
                                                               scale=1.0 / 128),
                                 r=[("ps", 7), ("cols", 2)], w=[("rstd",)])

                        def stE2():
                            R.op("act", lambda e: e.activation(out=rstd[:, 1:TW + 1], in_=rstd[:, 1:TW + 1], func=AF.Exp,
                                                               bias=zero_c, scale=-0.5),
                                 r=[("rstd",), ("cols",)], w=[("rstd",)])

                        def stF(oh=oh, h=h, ohk=ohk):
                            R.op("dve", lambda e: e.scalar_tensor_tensor(
                                out=hn[:, h, 1:TW + 1], in0=oh, scalar=subg_c, in1=rstd[:, 1:TW + 1], op0=ALU.mult, op1=ALU.mult),
                                r=ohk + [("cols", 4), ("rstd",)], w=[("hn", h)])
                        stages += [stC, (lambda: (stD(), stE())), stE2, stF]
                    for n_, f_ in enumerate(stages):
                        pending.append((i + 2 + 2 * n_, f_))

                pending = []
                for i in range(len(steps) + LA):
                    if i < len(steps):
                        s_stage(i)
                    if i - LA >= 0:
                        av_stage(i - LA)
                        while pending and pending[0][0] <= i - LA:
                            pending.pop(0)[1]()
                while pending:
                    pending.pop(0)[1]()
                if DEBUG["Bstage"] < 2:
                    R.op("sp", lambda e, t=t: e.dma_start(out=outT[t].rearrange("p (k n) -> p k n", k=KC)[:, :, 0:257], in_=hn[:, :, 0:514].bitcast(F32)),
                         r=[("hn", kc) for kc in range(KC)], w=[("out", t)], dma=True)
                    continue
                R.op("sp", lambda e, t=t: e.dma_start(out=xh[:, :, 1:TW + 1], in_=h1s[t].rearrange("p (k n) -> p k n", k=KC)),
                     r=[("h1s", t)], w=[("xh", kc) for kc in range(KC)], dma=True)
                resid_linear(P_WO, 1, lambda gk: hn[:, gk, 1:TW + 1], lambda gk: ("hn", gk))
                rmsnorm(3, False)
                mlp(P_MLP1)
                rmsnorm(4, False, write_hn=False)
                for kc in range(KC):
                    R.op("dve", lambda e, kc=kc: e.scalar_tensor_tensor(
                        out=xh[:, kc, 1:TW + 1], in0=xh[:, kc, 1:TW + 1], scalar=gsb[:, 4 * KC + kc:4 * KC + kc + 1],
                        in1=rstd[:, 1:TW + 1], op0=ALU.mult, op1=ALU.mult),
                        r=[("xh", kc), ("gsb",), ("rstd",)], w=[("xh", kc)])
                R.op("sp", lambda e, t=t: e.dma_start(out=outT[t].rearrange("p (k n) -> p k n", k=KC), in_=xh[:, :, 1:TW + 1]),
                     r=[("xh", kc) for kc in range(KC)], w=[("out", t)], dma=True)

        pp_advance(NCH)
        assert wst["use"] == len(seq), (wst, len(seq))
        fin = Item("sp", None, set(R.dma_last.values()), False)
        for d in fin.deps:
            d.needed = True
        R.q["sp"].append(fin)
        with nc.Block() as block:
            R.emit(nc, block, semh)
        print("SEMSTAT", mode, {e: (len(R.q[e]), max([it.val for it in R.q[e] if not it.is_dma] + [0])) for e in Rec.ENGS},
              "dma", max(R.dma_cnt.values()), "sbuf", off[0] * 2)
    return nc


def _panels(W, r0s, c0s):
    out = np.empty((len(r0s), 128, 2048), np.float32)
    for i, (r0, c0) in enumerate(zip(r0s, c0s)):
        out[i] = W[r0:r0 + 2048, c0:c0 + 128].reshape(16, 128, 128).transpose(1, 0, 2).reshape(128, 2048)
    return out


def _mlp_panels(w1, w2):
    ps = []
    for half in range(2):
        cs = [(half * 32 + fo) * 128 for fo in range(32)]
        ps.append(_panels(w1, [0] * 32, cs))
        r0s, c0s = [], []
        for do in range(16):
            for s in range(2):
                r0s.append(half * 4096 + s * 2048)
                c0s.append(do * 128)
        ps.append(_panels(w2, r0s, c0s))
    return np.concatenate(ps, 0)


def _layout_inputs(inp):
    x = np.asarray(inp["x"], np.float32)
    w_in = np.asarray(inp["conv_w_in"][0], np.float32)
    cs = []
    for j in range(16):
        cs += [(16 + j) * 128, (32 + j) * 128, j * 128]
    pan = [_panels(w_in, [0] * 48, cs)]
    pan.append(_panels(np.asarray(inp["conv_w_out"][0], np.float32), [0] * 16, [d * 128 for d in range(16)]))
    pan.append(_mlp_panels(np.asarray(inp["mlp_w1"][0], np.float32), np.asarray(inp["mlp_w2"][0], np.float32)))
    cs = []
    for h in range(16):
        cs += [h * 128, (16 + h) * 128, (32 + h) * 128]
    pan.append(_panels(np.asarray(inp["attn_w_qkv"][0], np.float32), [0] * 48, cs))
    pan.append(_panels(np.asarray(inp["attn_w_o"][0], np.float32), [0] * 16, [d * 128 for d in range(16)]))
    pan.append(_mlp_panels(np.asarray(inp["mlp_w1"][1], np.float32), np.asarray(inp["mlp_w2"][1], np.float32)))
    wpan = np.concatenate(pan, 0)
    assert wpan.shape[0] == NPAN
    fm = lambda v: np.asarray(v, np.float32).reshape(16, 128).T
    gains = np.concatenate([fm(inp["ln_mix"][0]), fm(inp["ln_mlp"][0]), fm(inp["ln_mix"][1]),
                            fm(inp["ln_mlp"][1]), fm(inp["ln_f"])], 1)
    cw = np.asarray(inp["conv_w"][0], np.float32)
    convw = np.concatenate([fm(cw[0]), fm(cw[1]), fm(cw[2])], 1)
    subln = np.asarray(inp["attn_subln"][0], np.float32).reshape(128, 1)
    lamp = np.stack([np.asarray(inp[k][0], np.float32) for k in
                     ("attn_lambda_q1", "attn_lambda_k1", "attn_lambda_q2", "attn_lambda_k2")], 0)
    per_core = []
    xp = np.zeros((4, SEQ + 2, D), np.float32)
    xp[:, 1:SEQ + 1] = x
    for c in range(NCORE):
        b, hf = c // 2, c % 2
        xt = np.empty((NT, 128, KC, TWH), np.float32)
        for t in range(NT):
            s0 = hf * HALF + t * TW
            blk = xp[b, s0:s0 + TWH]
            xt[t] = blk.T.reshape(KC, 128, TWH).transpose(1, 0, 2)
        smalls = np.concatenate([convw, subln, np.full((128, 1), hf * HALF, np.float32)], 1)
        per_core.append({"xT": xt.reshape(NT, 128, KC * TWH), "smalls": np.ascontiguousarray(smalls)})
    return wpan, np.ascontiguousarray(gains), np.ascontiguousarray(lamp), per_core


def _assemble(outs):
    out = np.empty((4, SEQ, D), np.float32)
    for c in range(NCORE):
        b, hf = c // 2, c % 2
        o = outs[c].reshape(NT, 128, KC, TW)
        for t in range(NT):
            s0 = hf * HALF + t * TW
            out[b, s0:s0 + TW] = o[t].transpose(2, 1, 0).reshape(TW, D)
    return out


FUSED = True
DEBUG = {"NT": NT, "stop": 99, "Bstage": 2, "NHL": NH, "v": 0}
_cache = {}


def _get(mode):
    if mode not in _cache:
        _cache[mode] = build(mode)
    return _cache[mode]


def kernel(**inp):
    wpan, gains, lamp, pc = _layout_inputs(inp)
    cores = list(range(NCORE))
    if FUSED:
        nc = _get("ALL")
        maps = [{"wpan": wpan, "gains": gains, "lamp": lamp, "smalls": pc[c]["smalls"], "xT": pc[c]["xT"]}
                for c in cores]
        res = run_bass_kernel_spmd(nc, maps, core_ids=cores)
        return _assemble([np.asarray(r["outT"]) for r in res.results])
    ncA = _get("A")
    wA = np.ascontiguousarray(wpan[:P_WO])
    maps = [{"wpan": wA, "gains": gains, "lamp": lamp, "smalls": pc[c]["smalls"], "xT": pc[c]["xT"]} for c in cores]
    ra = run_bass_kernel_spmd(ncA, maps, core_ids=cores).results
    ncB = _get("B")
    wB = np.ascontiguousarray(wpan[P_WO:])
    maps = []
    for c in cores:
        c0 = (c // 2) * 2
        Kf = np.stack([np.asarray(ra[c0]["Ks"]), np.asarray(ra[c0 + 1]["Ks"])], 0)
        Vf = np.stack([np.asarray(ra[c0]["Vs"]), np.asarray(ra[c0 + 1]["Vs"])], 0)
        maps.append({"wpan": wB, "gains": gains, "lamp": lamp, "smalls": pc[c]["smalls"],
                     "h1s": np.asarray(ra[c]["h1s"]), "Qs": np.asarray(ra[c]["Qs"]), "Kf": Kf, "Vf": Vf})
    rb = run_bass_kernel_spmd(ncB, maps, core_ids=cores).results
    return _assemble([np.asarray(r["outT"]) for r in rb])
```
